# Optimizing a Trainium2 kernel written in Bass

```python
import jax, jax.numpy as jnp
from jax import lax
import numpy as np

D_MODEL = 1024
BATCH = 32
SEQ = 256
DEPTH = 4
DEC_BATCH = 2
DEC_SEQ = 1024
PAST_LEN = 256

GRID_W = 64
N_MIXERS = 2
EXPAND = 2
D_INNER = EXPAND * D_MODEL
CONV_K = 31
RET_HEADS = 4
RET_DK = D_MODEL // RET_HEADS
RET_DV = D_INNER // RET_HEADS
CHUNK = 128
ROPE_BASE = 10000.0
EPS = 1e-6
N_CONV = (DEPTH + 1) // 2
N_RET = DEPTH // 2

kernel_name = "conformer_retention_prefix_dit"


def rms_norm(x, g):
    xf = x.astype(jnp.float32)
    y = xf * lax.rsqrt(jnp.mean(xf * xf, axis=-1, keepdims=True) + EPS)
    return (y * g.astype(jnp.float32)).astype(x.dtype)


def layer_norm(x, g, b):
    xf = x.astype(jnp.float32)
    mu = jnp.mean(xf, axis=-1, keepdims=True)
    var = jnp.mean(jnp.square(xf - mu), axis=-1, keepdims=True)
    y = (xf - mu) * lax.rsqrt(var + EPS) * g.astype(jnp.float32) + b.astype(jnp.float32)
    return y.astype(x.dtype)


def head_norm(o, g):
    of = o.astype(jnp.float32)
    mu = jnp.mean(of, axis=-1, keepdims=True)
    var = jnp.mean(jnp.square(of - mu), axis=-1, keepdims=True)
    y = (of - mu) * lax.rsqrt(var + EPS) * g.astype(jnp.float32).reshape(RET_HEADS, RET_DV)
    return y.astype(o.dtype)


def adaln(cond, w, b):
    m = jax.nn.silu(cond) @ w + b
    return jnp.split(m, 3, axis=-1)


def rope_2d(L, dtype):
    rows = L // GRID_W
    r = jnp.repeat(jnp.arange(rows, dtype=jnp.float32), GRID_W)
    col = jnp.tile(jnp.arange(GRID_W, dtype=jnp.float32), rows)
    nf = RET_DK // 4
    inv = ROPE_BASE ** (-jnp.arange(nf, dtype=jnp.float32) / nf)
    ang = jnp.concatenate([r[:, None] * inv, col[:, None] * inv], axis=-1)
    return jnp.cos(ang).astype(dtype), jnp.sin(ang).astype(dtype)


def apply_rope(x, cos, sin):
    half = RET_DK // 2
    x1, x2 = x[..., :half], x[..., half:]
    cs, sn = cos[None, :, None, :], sin[None, :, None, :]
    return jnp.concatenate([x1 * cs - x2 * sn, x2 * cs + x1 * sn], axis=-1)


def conv_mixer(h, w_in, dw, dw_b, ln_g, ln_b, w_out):
    a, b, g = jnp.split(h @ w_in, 3, axis=-1)
    u = a * jax.nn.sigmoid(b)
    u = lax.conv_general_dilated(
        u, dw[:, None, :].astype(u.dtype), window_strides=(1,),
        padding=[(CONV_K // 2, CONV_K // 2)],
        dimension_numbers=('NWC', 'WIO', 'NWC'),
        feature_group_count=D_INNER) + dw_b
    u = jax.nn.silu(layer_norm(u, ln_g, ln_b))
    return (u * jax.nn.silu(g)) @ w_out


def retention_scan(q, k, v, log_gamma, s0):
    B, L = q.shape[0], q.shape[1]
    n = L // CHUNK
    dt = q.dtype
    qc = q.reshape(B, n, CHUNK, RET_HEADS, RET_DK)
    kc = k.reshape(B, n, CHUNK, RET_HEADS, RET_DK)
    vc = v.reshape(B, n, CHUNK, RET_HEADS, RET_DV)
    pos = jnp.arange(CHUNK, dtype=jnp.float32)
    diff = pos[:, None] - pos[None, :]
    decay_in = jnp.where(diff[..., None] >= 0,
                         jnp.exp(jnp.maximum(diff, 0.0)[..., None] * log_gamma), 0.0)
    scores = jnp.einsum('bnihk,bnjhk->bnhij', qc, kc) * jnp.transpose(decay_in, (2, 0, 1)).astype(dt)
    o_in = jnp.einsum('bnhij,bnjhv->bnihv', scores, vc)
    w_k = jnp.exp((CHUNK - 1 - pos)[:, None] * log_gamma).astype(dt)
    kv = jnp.einsum('bnjhk,jh,bnjhv->bnhkv', kc, w_k, vc)
    g_chunk = jnp.exp(CHUNK * log_gamma).astype(dt)[None, :, None, None]

    def step(s, kv_n):
        return g_chunk * s + kv_n, s

    s_fin, s_prev = lax.scan(step, s0.astype(dt), jnp.moveaxis(kv, 1, 0))
    s_prev = jnp.moveaxis(s_prev, 0, 1)
    w_q = jnp.exp((pos + 1.0)[:, None] * log_gamma).astype(dt)
    o_x = jnp.einsum('bnihk,ih,bnhkv->bnihv', qc, w_q, s_prev)
    return (o_in + o_x).reshape(B, L, RET_HEADS, RET_DV), s_fin


def retention_mixer(h, w_in, log_decay, gn_g, w_out, s_init, rope):
    B, L = h.shape[0], h.shape[1]
    q, k, v, g = jnp.split(h @ w_in, [D_MODEL, 2 * D_MODEL, 2 * D_MODEL + D_INNER], axis=-1)
    q = q.reshape(B, L, RET_HEADS, RET_DK)
    k = k.reshape(B, L, RET_HEADS, RET_DK) * (RET_DK ** -0.5)
    v = v.reshape(B, L, RET_HEADS, RET_DV)
    if rope is not None:
        q = apply_rope(q, *rope)
        k = apply_rope(k, *rope)
    if s_init is None:
        s_init = jnp.zeros((B, 2, RET_HEADS, RET_DK, RET_DV), q.dtype)
    log_gamma = -jnp.exp(log_decay.astype(jnp.float32))
    o_f, s_f = retention_scan(q, k, v, log_gamma[0], s_init[:, 0])
    o_b, s_b = retention_scan(q[:, ::-1], k[:, ::-1], v[:, ::-1], log_gamma[1], s_init[:, 1])
    o = head_norm(o_f + o_b[:, ::-1], gn_g).reshape(B, L, D_INNER)
    return (o * jax.nn.silu(g)) @ w_out, jnp.stack([s_f, s_b], axis=1)


def setup_inputs(seed: int = 0) -> dict:
    key = jax.random.key(seed)
    ks = jax.random.split(key, 24)
    f32 = jnp.float32
    nrm = lambda k, shape, s: jax.random.normal(k, shape, f32) * s
    base_decay = np.log(-np.log(1.0 - 2.0 ** (-5.0 - np.arange(RET_HEADS)))).astype(np.float32)
    return {
        "x_prompt": nrm(ks[0], (BATCH, SEQ, D_MODEL), 1.0),
        "x_sample": nrm(ks[1], (DEC_BATCH, DEC_SEQ, D_MODEL), 1.0),
        "c": nrm(ks[2], (DEC_BATCH, D_MODEL), 1.0),
        "state_ret": nrm(ks[3], (DEC_BATCH, N_RET, 2, RET_HEADS, RET_DK, RET_DV), 0.5),
        "c_ctx": nrm(ks[4], (D_MODEL,), 1.0),
        "ada_w": nrm(ks[5], (DEPTH, D_MODEL, 3 * D_MODEL), D_MODEL ** -0.5),
        "ada_b": nrm(ks[6], (DEPTH, 3 * D_MODEL), 0.01),
        "norm_g": 1.0 + nrm(ks[7], (DEPTH, D_MODEL), 0.01),
        "final_norm_g": 1.0 + nrm(ks[8], (D_MODEL,), 0.01),
        "conv_w_in": nrm(ks[9], (N_CONV, D_MODEL, 3 * D_INNER), D_MODEL ** -0.5),
        "conv_dw": nrm(ks[10], (N_CONV, CONV_K, D_INNER), CONV_K ** -0.5),
        "conv_dw_b": nrm(ks[11], (N_CONV, D_INNER), 0.01),
        "conv_ln_g": 1.0 + nrm(ks[12], (N_CONV, D_INNER), 0.01),
        "conv_ln_b": nrm(ks[13], (N_CONV, D_INNER), 0.01),
        "conv_w_out": nrm(ks[14], (N_CONV, D_INNER, D_MODEL), D_INNER ** -0.5),
        "ret_w_in": nrm(ks[15], (N_RET, D_MODEL, 2 * D_MODEL + 2 * D_INNER), D_MODEL ** -0.5),
        "ret_log_decay": jnp.asarray(base_decay)[None, None, :] + nrm(ks[16], (N_RET, 2, RET_HEADS), 0.1),
        "ret_gn_g": 1.0 + nrm(ks[17], (N_RET, D_INNER), 0.01),
        "ret_w_out": nrm(ks[18], (N_RET, D_INNER, D_MODEL), D_INNER ** -0.5),
    }


def reference(x_prompt, x_sample, c, state_ret, c_ctx, ada_w, ada_b, norm_g, final_norm_g,
              conv_w_in, conv_dw, conv_dw_b, conv_ln_g, conv_ln_b, conv_w_out,
              ret_w_in, ret_log_decay, ret_gn_g, ret_w_out):
    rope = rope_2d(x_sample.shape[1], x_sample.dtype)
    xp, xs = x_prompt, x_sample
    new_states = []
    for i in range(DEPTH):
        j = i // N_MIXERS
        sh_p, sc_p, gt_p = adaln(c_ctx[None, None, :], ada_w[i], ada_b[i])
        sh_s, sc_s, gt_s = adaln(c[:, None, :], ada_w[i], ada_b[i])
        hp = rms_norm(xp, norm_g[i]) * (1.0 + sc_p) + sh_p
        hs = rms_norm(xs, norm_g[i]) * (1.0 + sc_s) + sh_s
        if i % N_MIXERS == 0:
            op = conv_mixer(hp, conv_w_in[j], conv_dw[j], conv_dw_b[j], conv_ln_g[j], conv_ln_b[j], conv_w_out[j])
            os_ = conv_mixer(hs, conv_w_in[j], conv_dw[j], conv_dw_b[j], conv_ln_g[j], conv_ln_b[j], conv_w_out[j])
        else:
            op, s_ctx = retention_mixer(hp, ret_w_in[j], ret_log_decay[j], ret_gn_g[j], ret_w_out[j], None, None)
            os_, _ = retention_mixer(hs, ret_w_in[j], ret_log_decay[j], ret_gn_g[j], ret_w_out[j],
                                     state_ret[:, j], rope)
            new_states.append(s_ctx)
        xp = xp + gt_p * op
        xs = xs + gt_s * os_
    y_prompt = rms_norm(xp, final_norm_g)
    y_sample = rms_norm(xs, final_norm_g)
    new_state_ret = jnp.stack(new_states, axis=1)
    return (y_prompt, y_sample, new_state_ret)
```

```python
import numpy as np
from contextlib import ExitStack
import concourse.bass as bass
import concourse.mybir as mybir
from concourse.bass_utils import run_bass_kernel_spmd

F32 = mybir.dt.float32
BF16 = mybir.dt.bfloat16
AF = mybir.ActivationFunctionType
ALU = mybir.AluOpType

D = 1024
DI = 2048
T = 1280
SEG = 256
NSEG = 5
PAD = 15
KW = 31
H = 4
EPS = 1e-6
BLKS = [(0, 512), (512, 512), (1024, 256)]
SEGP = SEG + 2 * PAD
NDVE = 6


class Sched:
    ENG = ("pe", "act", "dve", "pool", "sp")

    def __init__(self, nc):
        self.nc = nc
        self.ops = {e: [] for e in self.ENG}
        self.sem = {e: nc.alloc_semaphore("s_" + e) for e in self.ENG}
        self.cnt = {e: 0 for e in self.ENG}
        self.dsem = {}
        self.dcnt = {}
        self.seen = {e: {} for e in self.ENG}
        self.last_w = {}
        self.reads = {}
        self.out_events = []
        self.nwaits = 0

    def _waits_for(self, eng, reads, writes):
        need = {}

        def add(ev):
            k, v = ev
            if need.get(k, 0) < v:
                need[k] = v

        for r in reads:
            if r in self.last_w:
                add(self.last_w[r])
        for w in writes:
            if w in self.last_w:
                add(self.last_w[w])
            for ev in self.reads.get(w, ()):
                add(ev)
        res = []
        seen = self.seen[eng]
        for k, v in need.items():
            if seen.get(k, 0) >= v:
                continue
            seen[k] = v
            res.append((k, v))
        self.nwaits += len(res)
        return res

    def _semobj(self, k):
        return self.sem[k] if k in self.sem else self.dsem[k]

    def _record(self, e, reads, writes):
        for r in reads:
            self.reads.setdefault(r, []).append(e)
        for w in writes:
            self.last_w[w] = e
            self.reads[w] = []

    def op(self, eng, fn, reads=(), writes=(), signal=True):
        pk = [r for r in reads if isinstance(r, tuple) and r[0] in ("ps", "psb")]
        if pk:
            writes = list(writes) + [r for r in pk if r not in writes]
        waits = self._waits_for(eng, reads, writes)
        if signal:
            self.cnt[eng] += 1
            self.ops[eng].append((waits, fn, (eng, 1)))
            self._record((eng, self.cnt[eng]), reads, writes)
        else:
            self.ops[eng].append((waits, fn, None))

    def dma(self, q, fn, semkey, reads=(), writes=(), is_output=False):
        if semkey not in self.dsem:
            self.dsem[semkey] = self.nc.alloc_semaphore("d_" + semkey)
            self.dcnt[semkey] = 0
        waits = self._waits_for(q, reads, writes)
        self.dcnt[semkey] += 16
        self.ops[q].append((waits, fn, (semkey, 16)))
        e = (semkey, self.dcnt[semkey])
        self._record(e, reads, writes)
        if is_output:
            self.out_events.append(e)

    def barrier(self):
        for e in ("pe", "act", "dve", "pool"):
            waits = []
            for o in ("pe", "act", "dve", "pool"):
                if o == e or self.cnt[o] == 0:
                    continue
                if self.seen[e].get(o, 0) < self.cnt[o]:
                    self.seen[e][o] = self.cnt[o]
                    waits.append((o, self.cnt[o]))
            if waits:
                self.ops[e].append((waits, None, None))

    def snapshot(self):
        return dict(self.cnt)

    def barrier_from(self, snap):
        comp = ("pe", "act", "dve", "pool")
        for e in comp:
            waits = []
            for o in comp:
                if o == e:
                    continue
                target = self.cnt[o] if o == "pe" else snap[o]
                if target > 0 and self.seen[e].get(o, 0) < target:
                    self.seen[e][o] = target
                    waits.append((o, target))
            if waits:
                self.ops[e].append((waits, None, None))

    def emit(self, block):
        S = self
        need = {}
        for (k, v) in self.out_events:
            need[k] = max(need.get(k, 0), v)
        final_waits = list(need.items())

        def run(engname, eng):
            for (waits, fn, inc) in S.ops[engname]:
                for (k, v) in waits:
                    eng.wait_ge(S._semobj(k), v)
                if fn is None:
                    continue
                ins = fn(eng)
                if inc is not None:
                    ins.then_inc(S._semobj(inc[0]), inc[1])
            if engname == "sp":
                for (k, v) in final_waits:
                    eng.wait_ge(S._semobj(k), v)

        @block.tensor
        def _(e):
            run("pe", e)

        @block.scalar
        def _(e):
            run("act", e)

        @block.vector
        def _(e):
            run("dve", e)

        @block.gpsimd
        def _(e):
            run("pool", e)

        @block.sync
        def _(e):
            run("sp", e)


def _param_layout():
    off = {}
    n = 0
    for name, w in [("adab", 96), ("ng", 32), ("fng", 8), ("dw", 2 * 16 * 31), ("dwb", 32), ("lng", 32),
                    ("lnb", 32), ("gng", 32), ("ld", 16), ("isS", 1), ("c127mp", 1), ("c255mp", 1), ("cp", 1),
                    ("c128pp", 1), ("dpos", 128), ("dneg", 128), ("mge", 128), ("mle", 128), ("rowi1", 128),
                    ("row128mi", 128), ("ident", 128)]:
        off[name] = (n, w)
        n += w
    return off, n


POFF, NPRM = _param_layout()


def _pack_params(inp, isS):
    P = np.zeros((128, NPRM), np.float32)

    def put(name, arr):
        o, w = POFF[name]
        P[:, o:o + w] = np.asarray(arr, np.float32).reshape(128, w)

    p = np.arange(128, dtype=np.float32)
    put("adab", inp["ada_b"].reshape(4, 24, 128).transpose(2, 0, 1))
    put("ng", inp["norm_g"].reshape(4, 8, 128).transpose(2, 0, 1))
    put("fng", inp["final_norm_g"].reshape(8, 128).transpose(1, 0))
    put("dw", inp["conv_dw"].reshape(2, 31, 16, 128).transpose(3, 0, 2, 1))
    put("dwb", inp["conv_dw_b"].reshape(2, 16, 128).transpose(2, 0, 1))
    put("lng", inp["conv_ln_g"].reshape(2, 16, 128).transpose(2, 0, 1))
    put("lnb", inp["conv_ln_b"].reshape(2, 16, 128).transpose(2, 0, 1))
    put("gng", inp["ret_gn_g"].reshape(2, 16, 128).transpose(2, 0, 1))
    put("ld", np.broadcast_to(inp["ret_log_decay"].reshape(1, 16), (128, 16)))
    put("isS", np.full((128, 1), isS))
    put("c127mp", 127 - p)
    put("c255mp", 255 - p)
    put("cp", p)
    put("c128pp", 128 + p)
    jj = p[:, None]
    ii = p[None, :]
    put("dpos", np.maximum(ii - jj, 0))
    put("dneg", np.maximum(jj - ii, 0))
    put("mge", (ii >= jj))
    put("mle", (jj >= ii))
    put("rowi1", np.broadcast_to(ii + 1, (128, 128)))
    put("row128mi", np.broadcast_to(128 - ii, (128, 128)))
    put("ident", np.eye(128))
    return P


def _rope_tables(is_sample):
    tab = np.zeros((128, 2, 1024), np.float32)
    if not is_sample:
        tab[:, 0, :] = 1.0
        return tab
    L = 1024
    GW = 64
    rows = L // GW
    r = np.repeat(np.arange(rows, dtype=np.float32), GW)
    col = np.tile(np.arange(GW, dtype=np.float32), rows)
    nf = 64
    inv = (np.float32(10000.0) ** (-np.arange(nf, dtype=np.float32) / np.float32(nf))).astype(np.float32)
    ang = np.concatenate([r[:, None] * inv, col[:, None] * inv], axis=-1).astype(np.float32)
    tab[:, 0, :] = np.cos(ang).T
    tab[:, 1, :] = np.sin(ang).T
    return tab


class _Stop(Exception):
    pass


_STOP = 99


def build_program(n_layers=4):
    nc = bass.Bass("TRN2", target_bir_lowering=False)

    def stage(k):
        if _STOP <= k:
            raise _Stop()

    def din(name, shape):
        return nc.dram_tensor(name, shape, F32, kind="ExternalInput").ap()

    xT_d = din("xT", [D, T])
    cond_d = din("cond", [128, 16])
    prm_d = din("prm", [128, NPRM])
    rope_d = din("rope", [128, 2, 1024])
    ext_d = din("ext", [2, 2, 4, 256, 512])
    adaw_d = din("adaw", [4, D, 3 * D])
    cwin_d = din("cwin", [2, 16, 128, 3072])
    cwout_d = din("cwout", [2, 8, 128, 2048])
    rwqk_d = din("rwqk", [2, 16, 128, 1024])
    rwvg_d = din("rwvg", [2, 8, 128, 4096])
    rwout_d = din("rwout", [2, 4, 128, 4096])
    yT_d = nc.dram_tensor("yT", [D, T], F32, kind="ExternalOutput").ap()
    st_d = nc.dram_tensor("st", [NSEG, 2, 2, 4, 256, 512], F32, kind="ExternalOutput").ap()

    with ExitStack() as es:
        def sb(name, shape, dt):
            return es.enter_context(nc.sbuf_tensor(name, shape, dt))

        XT = sb("XT", [128, 8, T], F32)
        HT = sb("HT", [128, 8, T], BF16)
        PRM = sb("PRM", [128, NPRM], F32)
        CONDT = sb("CONDT", [128, 16], F32)
        SCT = sb("SCT", [128, 8, 2], BF16)
        IDB = sb("IDB", [128, 128], BF16)
        ONESD = sb("ONESD", [128, 128], BF16)
        ONESI = sb("ONESI", [128, 128], BF16)
        MODS = [sb(f"MOD{i}", [128, 24, 2], F32) for i in range(4)]
        GMS = [sb(f"GM{i}", [128, 8, 2], F32) for i in range(4)]
        cur = {"l": 0}
        pending = []
        WA = [sb(f"WA{i}", [128, 2, 8, 128], BF16) for i in range(2)]
        WB = [sb(f"WB{i}", [128, 4096], BF16) for i in range(2)]
        ADA = [sb(f"ADA{i}", [128, 8, 128], BF16) for i in range(2)]
        RSB = sb("RSB", [128, 512], F32)
        TF = [sb(f"TF{i}", [128, 512], F32) for i in range(4)]
        TB = [sb(f"TB{i}", [128, 512], BF16) for i in range(4)]
        FST = [sb(f"FST{i}", [128, 2, 512], F32) for i in range(2)]
        SMALL = sb("SMALL", [128, 64], F32)
        LG = sb("LG", [128, 8], F32)
        COLS = sb("COLS", [128, 4, 8], F32)
        ARF = sb("ARF", [128, 3584], F32)
        ARN = 37376
        AR = sb("AR", [128, ARN], BF16)
        ps = [es.enter_context(nc.psum_tensor(f"ps{i}", [128, 512], F32)) for i in range(8)]
        psbs = [ps[6][:].bitcast(BF16), ps[7][:].bitcast(BF16)]
        PSTAT = 5

        s = Sched(nc)

        def P_(name, a=0, w=None):
            o, ww = POFF[name]
            if w is None:
                w = ww - a
            return PRM[:, o + a:o + a + w]

        def act(out, in_, func, reads, writes, scale=None, bias=None):
            kw = {}
            if scale is not None:
                kw["scale"] = scale
            if bias is not None:
                kw["bias"] = bias
            s.op("act", lambda e: e.activation(out=out, in_=in_, func=func, **kw), reads, writes)

        def tt(eng, out, in0, in1, op, reads, writes):
            s.op(eng, lambda e: e.tensor_tensor(out=out, in0=in0, in1=in1, op=op), reads, writes)

        def ts(eng, out, in0, s1, op0, reads, writes, s2=None, op1=None):
            if op1 is None:
                s.op(eng, lambda e: e.tensor_scalar(out=out, in0=in0, scalar1=s1, scalar2=None, op0=op0), reads, writes)
            else:
                s.op(eng, lambda e: e.tensor_scalar(out=out, in0=in0, scalar1=s1, scalar2=s2, op0=op0, op1=op1), reads, writes)

        def stt(out, in0, scalar, in1, op0, op1, reads, writes):
            s.op("dve", lambda e: e.scalar_tensor_tensor(out=out, in0=in0, scalar=scalar, in1=in1, op0=op0, op1=op1), reads, writes)

        def mmg(out, pairs, reads, writes, transpose=False):
            n = len(pairs)
            for i, (l, r) in enumerate(pairs):
                s.op("pe", (lambda e, l=l, r=r, i=i: e.matmul(out, lhsT=l, rhs=r, start=(i == 0), stop=(i == n - 1))),
                     reads, writes, signal=(i == n - 1))

        def tr(out, in_, reads, writes, signal=True):
            s.op("pe", lambda e: e.transpose(out=out, in_=in_, identity=IDB[:]), reads, writes, signal=signal)

        def cpy(eng, out, in_, reads, writes):
            if eng == "act":
                act(out, in_, AF.Copy, reads, writes)
            else:
                s.op(eng, lambda e: e.tensor_copy(out=out, in_=in_), reads, writes)

        def dma(q, out, in_, semkey, reads=(), writes=(), is_output=False, mdl=None):
            if mdl is None:
                s.dma(q, lambda e: e.dma_start(out=out, in_=in_), semkey, reads, writes, is_output)
            else:
                s.dma(q, lambda e: e.dma_start(out=out, in_=in_, max_dma_last_dim=mdl), semkey, reads, writes, is_output)

        ring = {"tf": 0, "tb": 0, "ps": 0, "psw": 0, "wide": False, "tfn": 4, "wn": 8, "nar": [0, 1, 2, 3, 4]}

        def tf():
            i = ring["tf"]
            i = i % ring["tfn"]
            ring["tf"] = (i + 1) % ring["tfn"]
            return TF[i], ("tf", i)

        def tb():
            i = ring["tb"]
            ring["tb"] = (i + 1) % 4
            return TB[i], ("tb", i)

        def psn():
            if ring["wide"]:
                i = ring["psw"]
                i = i % ring["wn"]
                ring["psw"] = (i + 1) % ring["wn"]
                return ps[i], ("ps", i)
            nar = ring["nar"]
            i = ring["ps"] % len(nar)
            ring["ps"] = (i + 1) % len(nar)
            return ps[nar[i]], ("ps", nar[i])

        dma("sp", PRM[:], prm_d[:], "prm", writes=["PRM"])
        dma("sp", CONDT[:], cond_d[:], "cond", writes=["CONDT"])
        xT_v = xT_d.rearrange("(kt p) t -> p kt t", p=128)
        for kt in range(8):
            dma("sp", XT[:, kt, :], xT_v[:, kt, :], f"x{kt}", writes=[("XT", kt, b) for b in range(3)])
        s.op("pool", lambda e: e.memset(ONESD[:], 1.0 / D), writes=["ONESD"])
        s.op("pool", lambda e: e.memset(ONESI[:], 1.0 / DI), writes=["ONESI"])
        cpy("dve", IDB[:], P_("ident"), ["PRM"], ["IDB"])
        act(SCT[:].rearrange("p k j -> p (k j)"), CONDT[:], AF.Silu, ["CONDT"], ["SCT"])

        def mod_group(l, ot):
            av = adaw_d[l].rearrange("(kt p) n -> p kt n", p=128)
            sl = ot % 2
            dma("pool", ADA[sl][:], av[:, :, ot * 128:(ot + 1) * 128], f"ada{sl}", writes=[("ADA", sl)])
            pm, pmk = psn()
            mmg(pm[:, 0:2], [(ADA[sl][:, kt, :], SCT[:, kt, :]) for kt in range(8)],
                reads=[("ADA", sl), "SCT"], writes=[pmk])
            o_, _w = POFF["adab"]
            adab = PRM[:, o_ + l * 24 + ot:o_ + l * 24 + ot + 1].broadcast_to([128, 2])
            tt("dve", MODS[l][:, ot, :], pm[:, 0:2], adab, ALU.add, [pmk, "PRM"], [("MOD", l)])

        def mod_finish(l):
            o_, _w = POFF["ng"]
            ngb = PRM[:, o_ + l * 8:o_ + (l + 1) * 8].unsqueeze(2).broadcast_to([128, 8, 2])
            ts("dve", GMS[l][:], MODS[l][:, 8:16, :], 1.0, ALU.add, [("MOD", l)], [("GM", l)])
            tt("dve", GMS[l][:], GMS[l][:], ngb, ALU.mult, [("GM", l), "PRM"], [("GM", l)])

        def mod_start(l):
            if l < n_layers:
                pending.extend((l, ot) for ot in range(24))

        def mod_drain(k):
            while k > 0 and pending:
                l, ot = pending.pop(0)
                mod_group(l, ot)
                if ot == 23:
                    mod_finish(l)
                k -= 1

        def SHc(kt, cj):
            return MODS[cur["l"]][:, kt, cj:cj + 1]

        def GTc(kt, cj):
            return MODS[cur["l"]][:, 16 + kt, cj:cj + 1]

        def rms_block(bi, gfun, outfun):
            t0, n = BLKS[bi]
            for kt in range(8):
                sq, sqk = tb()
                act(sq[:, :n], XT[:, kt, t0:t0 + n], AF.Square, [("XT", kt, bi)], [sqk])
                s.op("pe", (lambda e, kt=kt, sq=sq: e.matmul(ps[PSTAT][:, :n], lhsT=ONESD[:], rhs=sq[:, :n], start=(kt == 0), stop=(kt == 7))),
                     [sqk, "ONESD"], [("ps", PSTAT)], signal=True)
            rs, rsk = RSB, "RSB"
            ts("dve", rs[:, :n], ps[PSTAT][:, :n], EPS, ALU.add, [("ps", PSTAT)], [rsk])
            act(rs[:, :n], rs[:, :n], AF.Sqrt, [rsk], [rsk])
            s.op("dve", lambda e: e.reciprocal(out=rs[:, :n], in_=rs[:, :n]), [rsk], [rsk])
            for kt in range(8):
                tmp, tk = tf()
                stt(tmp[:, :n], XT[:, kt, t0:t0 + n], gfun(kt), rs[:, :n], ALU.mult, ALU.mult,
                    [("XT", kt, bi), rsk, ("GM", cur["l"]), "PRM"], [tk])
                outfun(kt, tmp, tk)

        def rms_to_HT_blk(bi):
            t0, n = BLKS[bi]
            cj = 0 if bi < 2 else 1

            def outf(kt, tmp, tk):
                act(HT[:, kt, t0:t0 + n], tmp[:, :n], AF.Identity, [tk, ("MOD", cur["l"])], [("HT", kt, bi)],
                    bias=SHc(kt, cj), scale=1.0)

            rms_block(bi, lambda kt: GMS[cur["l"]][:, kt, cj:cj + 1], outf)

        def rms_to_HT():
            arfk = [("RS", b_) for b_ in range(3)] + [("NMR", b_) for b_ in range(3)]
            RSv = [ARF[:, b_ * 512:(b_ + 1) * 512] for b_ in range(3)]
            banks = [(ps[PSTAT], ("ps", PSTAT)), psn(), psn()]
            for bi, (t0, n) in enumerate(BLKS):
                pb_, pbk_ = banks[bi]
                for kt in range(8):
                    sq, sqk = tb()
                    act(sq[:, :n], XT[:, kt, t0:t0 + n], AF.Square, [("XT", kt, bi)], [sqk])
                    s.op("pe", (lambda e, kt=kt, sq=sq, n=n, pb_=pb_: e.matmul(pb_[:, :n], lhsT=ONESD[:], rhs=sq[:, :n], start=(kt == 0), stop=(kt == 7))),
                         [sqk, "ONESD"], [pbk_], signal=True)
            for bi, (t0, n) in enumerate(BLKS):
                pb_, pbk_ = banks[bi]
                rs = RSv[bi]
                ts("dve", rs[:, :n], pb_[:, :n], EPS, ALU.add, [pbk_], arfk)
                act(rs[:, :n], rs[:, :n], AF.Sqrt, arfk, arfk)
                s.op("dve", (lambda e, rs=rs, n=n: e.reciprocal(out=rs[:, :n], in_=rs[:, :n])), arfk, arfk)
            for bi, (t0, n) in enumerate(BLKS):
                cj = 0 if bi < 2 else 1
                rs = RSv[bi]
                for kt in range(8):
                    tmp, tk = tf()
                    stt(tmp[:, :n], XT[:, kt, t0:t0 + n], GMS[cur["l"]][:, kt, cj:cj + 1], rs[:, :n], ALU.mult, ALU.mult,
                        [("XT", kt, bi), ("GM", cur["l"]), "PRM"] + arfk, [tk])
                    act(HT[:, kt, t0:t0 + n], tmp[:, :n], AF.Identity, [tk, ("MOD", cur["l"])], [("HT", kt, bi)],
                        bias=SHc(kt, cj), scale=1.0)

        def final_norm_blk(bi):
            yv = yT_d.rearrange("(kt p) t -> p kt t", p=128)
            o_, _w = POFF["fng"]
            t0, n = BLKS[bi]

            def outf(kt, tmp, tk):
                dma("sp", yv[:, kt, t0:t0 + n], tmp[:, :n], f"y{tk[1]}", reads=[tk], is_output=True)

            rms_block(bi, lambda kt: PRM[:, o_ + kt:o_ + kt + 1], outf)

        def next_rms(bi):
            l = cur["l"]
            if l + 1 < n_layers:
                assert not [p_ for p_ in pending if p_[0] == l + 1]
                cur["l"] = l + 1
                rms_to_HT_blk(bi)
                cur["l"] = l
            else:
                final_norm_blk(bi)

        prep_done = set()
        snap = {"s": None}

        def qk_dma(j, h):
            for which in range(2):
                for dkt in range(2):
                    ot = which * 8 + h * 2 + dkt
                    if which == 0:
                        dma("pool", WA[dkt][:, 0].rearrange("p k m -> p (k m)"), rwqk_d[j, ot], f"wA{dkt}",
                            writes=[("wA", dkt)], mdl=4096)
                    else:
                        dma("pool", ADA[dkt][:].rearrange("p k m -> p (k m)"), rwqk_d[j, ot], f"ada{dkt}",
                            writes=[("ADA", dkt)], mdl=4096)

        def ret_prep(j):
            prep_done.add(j)
            COS = ARF[:, 0:1024]
            SIN = ARF[:, 1024:2048]
            DT = ARF[:, 2048:2560].rearrange("p (h i) -> p h i", h=4)
            ARF16 = ARF[:].bitcast(BF16)
            WQF = ARF16[:, 5120:5632].rearrange("p (h i) -> p h i", h=4)
            WQB = ARF16[:, 5632:6144].rearrange("p (h i) -> p h i", h=4)
            isS = P_("isS")
            qk_dma(j, 0)
            dma("sp", ARF[:, 0:2048], rope_d.rearrange("p a t -> p (a t)"), "rope",
                writes=["rope"] + [("RS", b) for b in range(3)] + [("NMR", b) for b in range(3)])
            old, _w = POFF["ld"]
            act(LG[:], PRM[:, old + j * 8:old + (j + 1) * 8], AF.Exp, ["PRM"], ["LG"])
            ts("dve", LG[:], LG[:], -1.0, ALU.mult, ["LG"], ["LG"])
            for h in range(H):
                lgf = LG[:, h:h + 1]
                lgb = LG[:, 4 + h:5 + h]
                e1, e1k = tf()
                e2, e2k = tf()
                act(e1[:, 0:128], P_("dpos"), AF.Exp, ["PRM", "LG"], [e1k], scale=lgf)
                tt("dve", e1[:, 0:128], e1[:, 0:128], P_("mge"), ALU.mult, [e1k, "PRM"], [e1k])
                act(e2[:, 0:128], P_("dneg"), AF.Exp, ["PRM", "LG"], [e2k], scale=lgb)
                tt("dve", e2[:, 0:128], e2[:, 0:128], P_("mle"), ALU.mult, [e2k, "PRM"], [e2k])
                tt("dve", DT[:, h, :], e1[:, 0:128], e2[:, 0:128], ALU.add, [e1k, e2k], ["DT"] + [("NMR", b_) for b_ in range(3)])
                act(WQF[:, h, :], P_("rowi1"), AF.Exp, ["PRM", "LG"], ["WQ"], scale=lgf)
                act(WQB[:, h, :], P_("row128mi"), AF.Exp, ["PRM", "LG"], ["WQ"], scale=lgb)
                act(COLS[:, h, 0:1], P_("c127mp"), AF.Exp, ["PRM", "LG"], ["COLS"], scale=lgf)
                act(COLS[:, h, 1:2], P_("c255mp"), AF.Exp, ["PRM", "LG"], ["COLS"], scale=lgf)
                act(COLS[:, h, 2:3], P_("cp"), AF.Exp, ["PRM", "LG"], ["COLS"], scale=lgb)
                act(COLS[:, h, 3:4], P_("c128pp"), AF.Exp, ["PRM", "LG"], ["COLS"], scale=lgb)
                act(COLS[:, h, 4:5], lgf, AF.Exp, ["LG"], ["COLS"], scale=128.0)
                act(COLS[:, h, 5:6], lgf, AF.Exp, ["LG"], ["COLS"], scale=256.0)
                act(COLS[:, h, 6:7], lgb, AF.Exp, ["LG"], ["COLS"], scale=128.0)
                act(COLS[:, h, 7:8], lgb, AF.Exp, ["LG"], ["COLS"], scale=256.0)
                ts("dve", COLS[:, h, 4:8], COLS[:, h, 4:8], isS, ALU.mult, ["COLS", "PRM"], ["COLS"])

        def conv_layer(l, j):
            C = AR[:, 0:16 * T].rearrange("p (c t) -> p c t", c=16)
            DG = [AR[:, 20480 + i * 3968:20480 + (i + 1) * 3968].rearrange("p (k m) -> p k m", k=KW) for i in range(2)]
            UP = [AR[:, 28416 + i * 1430:28416 + (i + 1) * 1430].rearrange("p (s t) -> p s t", s=NSEG) for i in range(2)]
            RS = ARF[:, 0:T]
            NMR = ARF[:, T:2 * T]
            isS = P_("isS")
            for i in range(2):
                s.op("dve", (lambda e, i=i: e.memset(UP[i], 0.0)), writes=[("up", i)])
            odw, _w = POFF["dw"]
            odwb, _w = POFF["dwb"]
            dma("pool", WA[0][:].rearrange("p w k m -> p (w k m)"), cwin_d[j, 0, :, 0:2048], "wA0",
                writes=[("wA", 0)], mdl=4096)

            def ab(ct):
                sl = ct % 2
                if ct + 1 < 16:
                    dma("pool", WA[1 - sl][:].rearrange("p w k m -> p (w k m)"), cwin_d[j, ct + 1, :, 0:2048], f"wA{1 - sl}",
                        writes=[("wA", 1 - sl)], mdl=4096)
                dwc = PRM[:, odw + (j * 16 + ct) * KW + NDVE:odw + (j * 16 + ct + 1) * KW]
                tt("dve", DG[sl][:, NDVE:KW, :], IDB[:].unsqueeze(1).broadcast_to([128, KW - NDVE, 128]),
                   dwc.unsqueeze(2).broadcast_to([128, KW - NDVE, 128]), ALU.mult, ["IDB", "PRM"], [("dg", sl)])

            def ab_blk(ct, bi):
                sl = ct % 2
                if True:
                    t0, n = BLKS[bi]
                    ns = n // SEG
                    pa, pak = psn()
                    pb, pbk = psn()
                    hk = [("HT", kt, bi) for kt in range(8)]
                    mmg(pa[:, :n], [(WA[sl][:, 0, kt, :], HT[:, kt, t0:t0 + n]) for kt in range(8)], [("wA", sl)] + hk, [pak])
                    mmg(pb[:, :n], [(WA[sl][:, 1, kt, :], HT[:, kt, t0:t0 + n]) for kt in range(8)], [("wA", sl)] + hk, [pbk])
                    sg, sgk = tf()
                    act(sg[:, :n], pb[:, :n], AF.Sigmoid, [pbk], [sgk])
                    tt("dve", UP[sl][:, 2 * bi:2 * bi + ns, PAD:PAD + SEG],
                       pa[:, :n].rearrange("p (s t) -> p s t", s=ns), sg[:, :n].rearrange("p (s t) -> p s t", s=ns),
                       ALU.mult, [pak, sgk], [("up", sl)])

            def pads(ct):
                sl = ct % 2
                for r in range(1, 4):
                    ts("dve", UP[sl][:, r, 0:PAD], UP[sl][:, r - 1, SEG:SEG + PAD], isS, ALU.mult, [("up", sl), "PRM"], [("up", sl)])
                for r in range(0, 3):
                    ts("dve", UP[sl][:, r, PAD + SEG:SEGP], UP[sl][:, r + 1, PAD:2 * PAD], isS, ALU.mult, [("up", sl), "PRM"], [("up", sl)])

            def dwconv_blk(ct, bi):
                sl = ct % 2
                for _ in range(1):
                    t0, n = BLKS[bi]
                    ns = n // SEG
                    pc, pck = psn()
                    mmg(pc[:, :n].rearrange("p (s t) -> p s t", s=ns),
                        [(DG[sl][:, k, :], UP[sl][:, 2 * bi:2 * bi + ns, k:k + SEG]) for k in range(NDVE, KW)],
                        [("dg", sl), ("up", sl)], [pck])
                    bcol = PRM[:, odwb + j * 16 + ct:odwb + j * 16 + ct + 1]
                    if NDVE == 0:
                        act(C[:, ct, t0:t0 + n], pc[:, :n], AF.Identity, [pck, "PRM"], [("C", ct, bi)], bias=bcol, scale=1.0)
                        continue
                    acc, acck = tf()
                    accv = acc[:, :n].rearrange("p (s t) -> p s t", s=ns)
                    for k in range(NDVE):
                        dcol = PRM[:, odw + (j * 16 + ct) * KW + k:odw + (j * 16 + ct) * KW + k + 1]
                        uv = UP[sl][:, 2 * bi:2 * bi + ns, k:k + SEG]
                        if k == 0:
                            ts("dve", accv, uv, dcol, ALU.mult, [("up", sl), "PRM"], [acck])
                        else:
                            stt(accv, uv, dcol, accv, ALU.mult, ALU.add, [("up", sl), "PRM", acck], [acck])
                    stt(C[:, ct, t0:t0 + n], pc[:, :n], bcol, acc[:, :n], ALU.add, ALU.add, [pck, "PRM", acck], [("C", ct, bi)])

            ab(0)
            for bi in range(3):
                ab_blk(0, bi)
            pads(0)
            for ct in range(16):
                if ct + 1 < 16:
                    ab(ct + 1)
                for bi in range(3):
                    if ct + 1 < 16:
                        ab_blk(ct + 1, bi)
                        if bi == 2:
                            pads(ct + 1)
                    dwconv_blk(ct, bi)
                    if bi < 2:
                        mod_drain(1)
            sbank = [psn() for _ in range(3)]
            qbank = [psn() for _ in range(3)]
            for bi, (t0, n) in enumerate(BLKS):
                p1, p1k = sbank[bi]
                p2, p2k = qbank[bi]
                for ct in range(16):
                    s.op("pe", (lambda e, ct=ct, t0=t0, n=n, p1=p1: e.matmul(p1[:, :n], lhsT=ONESI[:], rhs=C[:, ct, t0:t0 + n], start=(ct == 0), stop=(ct == 15))),
                         [("C", ct, bi), "ONESI"], [p1k], signal=(ct == 15))
                for ct in range(16):
                    sq, sqk = tb()
                    if ct % 3 == 2:
                        tt("dve", sq[:, :n], C[:, ct, t0:t0 + n], C[:, ct, t0:t0 + n], ALU.mult, [("C", ct, bi)], [sqk])
                    else:
                        act(sq[:, :n], C[:, ct, t0:t0 + n], AF.Square, [("C", ct, bi)], [sqk])
                    s.op("pe", (lambda e, ct=ct, sq=sq, n=n, p2=p2: e.matmul(p2[:, :n], lhsT=ONESI[:], rhs=sq[:, :n], start=(ct == 0), stop=(ct == 15))),
                         [sqk, "ONESI"], [p2k], signal=True)
            for bi, (t0, n) in enumerate(BLKS):
                p1, p1k = sbank[bi]
                p2, p2k = qbank[bi]
                mu, muk = tf()
                cpy("dve", mu[:, :n], p1[:, :n], [p1k], [muk])
                m2, m2k = tf()
                tt("dve", m2[:, :n], mu[:, :n], mu[:, :n], ALU.mult, [muk], [m2k])
                tt("dve", m2[:, :n], p2[:, :n], m2[:, :n], ALU.subtract, [p2k, m2k], [m2k])
                ts("dve", m2[:, :n], m2[:, :n], 0.0, ALU.max, [m2k], [m2k], s2=EPS, op1=ALU.add)
                act(m2[:, :n], m2[:, :n], AF.Sqrt, [m2k], [m2k])
                s.op("dve", (lambda e, m2=m2, t0=t0, n=n: e.reciprocal(out=RS[:, t0:t0 + n], in_=m2[:, :n])), [m2k], [("RS", bi)])
                stt(NMR[:, t0:t0 + n], mu[:, :n], -1.0, RS[:, t0:t0 + n], ALU.mult, ALU.mult, [muk, ("RS", bi)], [("NMR", bi)])
            WT = [(WB[0][:, 0:2048], ("wB", 0)), (WB[0][:, 2048:4096], ("wB", 0)),
                  (WB[1][:, 0:2048], ("wB", 1)), (WB[1][:, 2048:4096], ("wB", 1)),
                  (AR[:, 31276:31276 + 2048], ("ARs", 0)), (AR[:, 31276 + 2048:31276 + 4096], ("ARs", 1)),
                  (AR[:, 20480:20480 + 2048], ("dg", 0)), (AR[:, 20480 + 3968:20480 + 3968 + 2048], ("dg", 1))]
            for ft in range(8):
                dma("pool", WT[ft][0], cwout_d[j, ft], f"wt{ft}", writes=[WT[ft][1]], mdl=4096)
            olng, _w = POFF["lng"]
            olnb, _w = POFF["lnb"]
            dma("pool", WA[0][:, 0].rearrange("p k m -> p (k m)"), cwin_d[j, 0, :, 2048:3072], "wA0",
                writes=[("wA", 0)], mdl=4096)
            pend_mult = []
            for ct in range(16):
                sl = ct % 2
                if ct + 1 < 16:
                    dma("pool", WA[1 - sl][:, 0].rearrange("p k m -> p (k m)"), cwin_d[j, ct + 1, :, 2048:3072], f"wA{1 - sl}",
                        writes=[("wA", 1 - sl)], mdl=4096)
                for bi, (t0, n) in enumerate(BLKS):
                    pg, pgk = psn()
                    hk = [("HT", kt, bi) for kt in range(8)]
                    mmg(pg[:, :n], [(WA[sl][:, 0, kt, :], HT[:, kt, t0:t0 + n]) for kt in range(8)], [("wA", sl)] + hk, [pgk])
                    sgb, sgbk = tb()
                    act(sgb[:, :n], pg[:, :n], AF.Silu, [pgk], [sgbk])
                    t1, t1k = tf()
                    tt("dve", t1[:, :n], C[:, ct, t0:t0 + n], RS[:, t0:t0 + n], ALU.mult, [("C", ct, bi), ("RS", bi)], [t1k])
                    tt("dve", t1[:, :n], t1[:, :n], NMR[:, t0:t0 + n], ALU.add, [t1k, ("NMR", bi)], [t1k])
                    yb, ybk = tb()
                    act(yb[:, :n], t1[:, :n], AF.Silu, [t1k, "PRM"], [ybk],
                        scale=PRM[:, olng + j * 16 + ct:olng + j * 16 + ct + 1],
                        bias=PRM[:, olnb + j * 16 + ct:olnb + j * 16 + ct + 1])
                    pend_mult.append((C[:, ct, t0:t0 + n], yb[:, :n], sgb[:, :n], [ybk, sgbk], [("C", ct, bi)]))
                    if len(pend_mult) > 1:
                        o_, a_, b_, r_, w_ = pend_mult.pop(0)
                        tt("dve", o_, a_, b_, ALU.mult, r_, w_)
            o_, a_, b_, r_, w_ = pend_mult.pop(0)
            tt("dve", o_, a_, b_, ALU.mult, r_, w_)
            if l + 1 < n_layers and (l + 1) // 2 not in prep_done:
                ret_prep((l + 1) // 2)
            mod_drain(99)
            snap["s"] = s.snapshot()
            for bi, (t0, n) in enumerate(BLKS):
                cj = 0 if bi < 2 else 1
                for ft in range(8):
                    wt, wtk = WT[ft]
                    wv = wt.rearrange("p (c m) -> p c m", c=16)
                    po, pok = psn()
                    mmg(po[:, :n], [(wv[:, c, :], C[:, c, t0:t0 + n]) for c in range(16)],
                        [wtk] + [("C", c, bi) for c in range(16)], [pok])
                    stt(XT[:, ft, t0:t0 + n], po[:, :n], GTc(ft, cj), XT[:, ft, t0:t0 + n], ALU.mult, ALU.add,
                        [pok, ("MOD", cur["l"]), ("XT", ft, bi)], [("XT", ft, bi)])
                next_rms(bi)

        def ret_layer(l, j):
            R1 = AR[:, 0:4 * T].rearrange("p (r t) -> p r t", r=4)
            V = AR[:, 5120:10240].rearrange("p (t v) -> p t v", t=10)
            SG = AR[:, 10240:15360].rearrange("p (t v) -> p t v", t=10)
            KS = AR[:, 15360:23040].rearrange("p (t w k) -> p t w k", t=10, w=3)
            BST = AR[:, 23040:32256].rearrange("p (r d v) -> p r d v", r=9, d=2)
            FBF = AR[:, 32256:36352].rearrange("p (r d v) -> p r d v", r=4, d=2)
            QFS = AR[:, 36352:36864].rearrange("p (d t) -> p d t", d=2)
            QBS = AR[:, 36864:37376].rearrange("p (d t) -> p d t", d=2)
            COS = ARF[:, 0:1024]
            SIN = ARF[:, 1024:2048]
            DT = ARF[:, 2048:2560].rearrange("p (h i) -> p h i", h=4)
            ARF16 = ARF[:].bitcast(BF16)
            WQF = ARF16[:, 5120:5632].rearrange("p (h i) -> p h i", h=4)
            WQB = ARF16[:, 5632:6144].rearrange("p (h i) -> p h i", h=4)
            isS = P_("isS")
            if j not in prep_done:
                ret_prep(j)
            stage(1)
            stage(2)
            ogn, _w = POFF["gng"]

            def a1_dma(h):
                qk_dma(j, h)

            def a1(h):
                dma("sp", FST[1][:], ext_d[j, 1, h].rearrange("(t p) v -> p t v", p=128), "extb",
                    writes=[("FST", 1, 0), ("FST", 1, 1)])
                dma("sp", FST[0][:], ext_d[j, 0, h].rearrange("(t p) v -> p t v", p=128), "extf",
                    writes=[("FST", 0, 0), ("FST", 0, 1)])
                for which in range(2):
                    kscale = 1.0 if which == 0 else 0.0625
                    for bi, (t0, n) in enumerate(BLKS):
                        hk = [("HT", kt, bi) for kt in range(8)]
                        p1, p1k = psn()
                        p2, p2k = psn()
                        if which == 0:
                            w1, w2, wk1, wk2 = WA[0][:, 0], WA[1][:, 0], ("wA", 0), ("wA", 1)
                        else:
                            w1, w2, wk1, wk2 = ADA[0], ADA[1], ("ADA", 0), ("ADA", 1)
                        mmg(p1[:, :n], [(w1[:, kt, :], HT[:, kt, t0:t0 + n]) for kt in range(8)], [wk1] + hk, [p1k])
                        mmg(p2[:, :n], [(w2[:, kt, :], HT[:, kt, t0:t0 + n]) for kt in range(8)], [wk2] + hk, [p2k])
                        d1 = R1[:, which * 2 + 0, t0:t0 + n]
                        d2 = R1[:, which * 2 + 1, t0:t0 + n]
                        segs = [2 * bi, 2 * bi + 1] if bi < 2 else [4]
                        k1s = [("R1", which * 2 + 0, sg_) for sg_ in segs]
                        k2s = [("R1", which * 2 + 1, sg_) for sg_ in segs]
                        if bi < 2:
                            a_, ak = tf()
                            b_, bk = tf()
                            stt(a_[:, :n], p1[:, :n], kscale, COS[:, t0:t0 + n], ALU.mult, ALU.mult, [p1k, "rope"], [ak])
                            stt(b_[:, :n], p2[:, :n], kscale, SIN[:, t0:t0 + n], ALU.mult, ALU.mult, [p2k, "rope"], [bk])
                            tt("dve", d1, a_[:, :n], b_[:, :n], ALU.subtract, [ak, bk], k1s)
                            c_, ck = tf()
                            d_, dk_ = tf()
                            stt(c_[:, :n], p2[:, :n], kscale, COS[:, t0:t0 + n], ALU.mult, ALU.mult, [p2k, "rope"], [ck])
                            stt(d_[:, :n], p1[:, :n], kscale, SIN[:, t0:t0 + n], ALU.mult, ALU.mult, [p1k, "rope"], [dk_])
                            tt("dve", d2, c_[:, :n], d_[:, :n], ALU.add, [ck, dk_], k2s)
                        else:
                            act(d1, p1[:, :n], AF.Copy, [p1k], k1s, scale=kscale)
                            act(d2, p2[:, :n], AF.Copy, [p2k], k2s, scale=kscale)

            def wb_dma(h):
                for which in range(2):
                    dma("pool", WB[which][:], rwvg_d[j, which * 4 + h], f"wB{which}", writes=[("wB", which)], mdl=4096)

            a1(0)
            for h in range(H):
                stage(3)
                if h == 0:
                    wb_dma(0)

                def proj(which, tti):
                    sl = which
                    wv = WB[sl][:].rearrange("p (k n) -> p k n", k=8)
                    bi = tti // 4
                    hk = [("HT", kt, bi) for kt in range(8)]
                    pv, pvk = psn()
                    mmg(pv[:], [(HT[:, kt, tti * 128:(tti + 1) * 128], wv[:, kt, :]) for kt in range(8)], [("wB", sl)] + hk, [pvk])
                    if which == 0:
                        cpy("dve", V[:, tti, :], pv[:], [pvk], [("V", tti)])
                    else:
                        act(SG[:, tti, :], pv[:], AF.Silu, [pvk], [("SG", tti)])

                PTS = []
                for tti in range(10):
                    proj(0, tti)
                    rg = tti % 2
                    psb = psbs[rg]
                    for dkt in range(2):
                        tr(psb[:, dkt * 128:(dkt + 1) * 128], R1[:, 2 + dkt, tti * 128:(tti + 1) * 128],
                           [("R1", 2 + dkt, tti // 2), "IDB"], [("ps", 6 + rg)], signal=(dkt == 1))
                    src_ = psb[:, 0:256]
                    cols = [0, 1, 2] if tti % 2 == 0 else [0, 2, 3]
                    for w_, cidx in enumerate(cols):
                        col = COLS[:, h, cidx:cidx + 1]
                        act(KS[:, tti, w_, :], src_, AF.Identity, [("ps", 6 + rg), "COLS"], [("KS", tti)], scale=col)
                    cs = slice(tti * 128, (tti + 1) * 128)
                    psS, psSk = psn()
                    mmg(psS[:, 0:128], [(R1[:, 2 + dkt, cs], R1[:, dkt, cs]) for dkt in range(2)],
                        [("R1", i_, tti // 2) for i_ in range(4)], [psSk])
                    pt = TB[tti // 4][:, (tti % 4) * 128:(tti % 4 + 1) * 128]
                    ptk = ("tb", tti // 4)
                    tt("dve", pt, psS[:, 0:128], DT[:, h, :], ALU.mult, [psSk, "DT"], [ptk])
                    PTS.append((pt, ptk))
                    if tti < 6:
                        mod_drain(1)
                stage(5)

                def kv_group(c_first, w_first, c_second, w_second):
                    outp = []
                    for dkt in range(2):
                        pp, ppk = psn()
                        pairs = [(KS[:, c_first, w_first, dkt * 128:(dkt + 1) * 128], V[:, c_first, :])]
                        rd = [("KS", c_first), ("V", c_first)]
                        if c_second is not None:
                            pairs.append((KS[:, c_second, w_second, dkt * 128:(dkt + 1) * 128], V[:, c_second, :]))
                            rd += [("KS", c_second), ("V", c_second)]
                        mmg(pp[:], pairs, rd, [ppk])
                        outp.append((pp, ppk))
                    return outp

                ring["wide"] = True

                def st_out(r, d, FS):
                    dst = st_d[r, j, d, h].rearrange("(t p) v -> p t v", p=128)
                    for dkt in range(2):
                        dma("sp", dst[:, dkt, :], FS[:, dkt, :], f"st{d}{dkt}", reads=[("FST", d, dkt)], is_output=True)

                FF = FST[0]
                QF2 = TF[3][:].bitcast(BF16)
                QSETS = [(QFS, QBS, "Q0"), (QF2[:, 0:512].rearrange("p (d t) -> p d t", d=2),
                                            QF2[:, 512:1024].rearrange("p (d t) -> p d t", d=2), ("tf", 3))]

                def fwd_chain(r):
                    c0, c1 = 2 * r, 2 * r + 1
                    par = r % 2
                    pa = kv_group(c0, 0, None, None)
                    pbf = kv_group(c0, 1, c1, 0)
                    fin_k = ("FBF", par * 2)
                    fc1_k = ("FBF", par * 2 + 1)
                    for dkt in range(2):
                        fk = ("FST", 0, dkt)
                        if r < 4:
                            stt(FBF[:, par * 2 + 1, dkt, :], FF[:, dkt, :], COLS[:, h, 4:5], pa[dkt][0][:], ALU.mult, ALU.add,
                                [fk, "COLS", pa[dkt][1]], [fc1_k])
                        else:
                            cpy("act", FBF[:, par * 2 + 1, dkt, :], pa[dkt][0][:], [pa[dkt][1]], [fc1_k])
                    for dkt in range(2):
                        fk = ("FST", 0, dkt)
                        if r < 4:
                            stt(FF[:, dkt, :], FF[:, dkt, :], COLS[:, h, 5:6], pbf[dkt][0][:], ALU.mult, ALU.add,
                                [fk, "COLS", pbf[dkt][1]], [fk])
                        else:
                            cpy("dve", FF[:, dkt, :], pbf[dkt][0][:], [pbf[dkt][1]], [fk])
                    st_out(r, 0, FF)
                    qf, qb, qk = QSETS[par]
                    s0 = r * SEG
                    for dkt in range(2):
                        qv = R1[:, dkt, s0:s0 + SEG].rearrange("p (c i) -> p c i", c=2)
                        tt("dve", qf[:, dkt, :].rearrange("p (c i) -> p c i", c=2), qv,
                           WQF[:, h, :].unsqueeze(1).broadcast_to([128, 2, 128]), ALU.mult, [("R1", dkt, r), "WQ"], [qk])
                        tt("dve", qb[:, dkt, :].rearrange("p (c i) -> p c i", c=2), qv,
                           WQB[:, h, :].unsqueeze(1).broadcast_to([128, 2, 128]), ALU.mult, [("R1", dkt, r), "WQ"], [qk])

                def o_seg(r):
                    c0, c1 = 2 * r, 2 * r + 1
                    par = r % 2
                    fin_k = ("FBF", par * 2)
                    fc1_k = ("FBF", par * 2 + 1)
                    qf, qb, qk = QSETS[par]
                    for ci, tti in enumerate((c0, c1)):
                        pt, ptk = PTS[tti]
                        pairs = [(pt, V[:, tti, :])]
                        rd = [ptk, ("V", tti), qk]
                        if ci == 0:
                            sf = (FBF[:, par * 2], fin_k) if r < 4 else None
                            sbk = (BST[:, 2 * r + 1], ("BST", 2 * r + 1)) if r < 4 else (BST[:, 8], ("BST", 8))
                        else:
                            sf = (FBF[:, par * 2 + 1], fc1_k)
                            sbk = (BST[:, 2 * r], ("BST", 2 * r)) if r < 4 else None
                        if sf is not None:
                            pairs += [(qf[:, dkt, ci * 128:(ci + 1) * 128], sf[0][:, dkt, :]) for dkt in range(2)]
                            rd.append(sf[1])
                        if sbk is not None:
                            pairs += [(qb[:, dkt, ci * 128:(ci + 1) * 128], sbk[0][:, dkt, :]) for dkt in range(2)]
                            rd.append(sbk[1])
                        po, pok = psn()
                        mmg(po[:], pairs, rd, [pok])
                        b0 = (tti % 4) * 16
                        st6 = SMALL[:, b0:b0 + 6]
                        mv = SMALL[:, b0 + 6:b0 + 8]
                        rsd = SMALL[:, b0 + 8:b0 + 9]
                        smv = ("SMv", tti % 4)
                        smr = ("SMr", tti % 4)
                        s.op("dve", (lambda e, st6=st6, po=po: e.bn_stats(out=st6, in_=po[:])), [pok], [smv])
                        s.op("dve", (lambda e, st6=st6, mv=mv: e.bn_aggr(out=mv, in_=st6)), [smv], [smv])
                        ts("dve", rsd, mv[:, 1:2], EPS, ALU.add, [smv], [smr])
                        act(rsd, rsd, AF.Sqrt, [smr], [smr])
                        on, onk = tf()
                        stt(on[:], po[:], mv[:, 0:1], SG[:, tti, :], ALU.subtract, ALU.mult, [pok, smv, ("SG", tti)], [onk])
                        s.op("dve", (lambda e, rsd=rsd: e.reciprocal(out=rsd, in_=rsd)), [smr], [smr])
                        act(SG[:, tti, :], on[:], AF.Identity, [onk, smr], [("SG", tti)], scale=rsd)

                def zt(tti):
                    rg = tti % 2
                    psb = psbs[rg]
                    for c in range(4):
                        tr(psb[:, c * 128:(c + 1) * 128], SG[:, tti, c * 128:(c + 1) * 128],
                           [("SG", tti), "IDB"], [("ps", 6 + rg)], signal=(c == 3))
                    for c in range(4):
                        gc = PRM[:, ogn + j * 16 + h * 4 + c:ogn + j * 16 + h * 4 + c + 1]
                        act(V[:, tti, c * 128:(c + 1) * 128], psb[:, c * 128:(c + 1) * 128], AF.Identity,
                            [("ps", 6 + rg), "PRM"], [("V", tti)], scale=gc)

                def fin_copy(r):
                    par = r % 2
                    for dkt in range(2):
                        act(FBF[:, par * 2, dkt, :], FF[:, dkt, :], AF.Identity, [("FST", 0, dkt), "PRM"], [("FBF", par * 2)], scale=isS)

                FB = FST[1]
                gt = 0
                for dkt in range(2):
                    act(BST[:, 6, dkt, :], FB[:, dkt, :], AF.Identity, [("FST", 1, dkt), "PRM"], [("BST", 6)], scale=isS)
                for r in (3, 2, 1, 0):
                    proj(1, gt)
                    proj(1, gt + 1)
                    gt += 2
                    c0, c1 = 2 * r, 2 * r + 1
                    pa = kv_group(c1, 1, None, None)
                    pbb = kv_group(c1, 2, c0, 2)
                    for dkt in range(2):
                        fk = ("FST", 1, dkt)
                        stt(BST[:, 2 * r + 1, dkt, :], FB[:, dkt, :], COLS[:, h, 6:7], pa[dkt][0][:], ALU.mult, ALU.add,
                            [fk, "COLS", pa[dkt][1]], [("BST", 2 * r + 1)])
                    for dkt in range(2):
                        fk = ("FST", 1, dkt)
                        stt(FB[:, dkt, :], FB[:, dkt, :], COLS[:, h, 7:8], pbb[dkt][0][:], ALU.mult, ALU.add,
                            [fk, "COLS", pbb[dkt][1]], [fk])
                    st_out(r, 1, FB)
                    if r == 2:
                        fin_copy(0)
                        fwd_chain(0)
                        fin_copy(1)
                    if r > 0:
                        for dkt in range(2):
                            act(BST[:, 2 * (r - 1), dkt, :], FB[:, dkt, :], AF.Identity, [("FST", 1, dkt), "PRM"],
                                [("BST", 2 * (r - 1))], scale=isS)
                proj(1, 8)
                proj(1, 9)
                pa = kv_group(9, 1, None, None)
                pbb = kv_group(9, 2, 8, 2)
                for dkt in range(2):
                    cpy("act", BST[:, 8, dkt, :], pa[dkt][0][:], [pa[dkt][1]], [("BST", 8)])
                    cpy("dve", FB[:, dkt, :], pbb[dkt][0][:], [pbb[dkt][1]], [("FST", 1, dkt)])
                st_out(4, 1, FB)

                ring["tfn"] = 3
                ring["wn"] = 6
                if h + 1 < H:
                    a1_dma(h + 1)
                WOUT = AR[:, 15360:15360 + 4096]
                for r in range(NSEG):
                    if r + 1 < NSEG:
                        fwd_chain(r + 1)
                    if r + 1 == NSEG - 1:
                        dma("pool", WOUT, rwout_d[j, h], "wout", writes=[("KS", t_) for t_ in range(10)], mdl=4096)
                        if h + 1 < H:
                            wb_dma(h + 1)
                    o_seg(r)
                    if r + 2 <= 3:
                        fin_copy(r + 2)
                    if r >= 1:
                        zt(2 * r - 2)
                        zt(2 * r - 1)
                ring["wn"] = 8
                ring["tfn"] = 4
                ring["wide"] = False
                if h + 1 < H:
                    a1(h + 1)
                zt(8)
                zt(9)
                stage(7)
                stage(8)
                wv = WOUT.rearrange("p (c n) -> p c n", c=4)
                if h == H - 1:
                    mod_drain(99)
                    snap["s"] = s.snapshot()
                order = [(ft, bi) for ft in range(8) for bi in range(3)] if h < H - 1 else [(ft, bi) for bi in range(3) for ft in range(8)]
                for (ft, bi) in order:
                    if True:
                        t0, n = BLKS[bi]
                        cj = 0 if bi < 2 else 1
                        po, pok = psn()
                        nt = n // 128
                        tt0 = t0 // 128
                        mmg(po[:, :n].rearrange("p (t i) -> p t i", t=nt),
                            [(wv[:, c, ft * 128:(ft + 1) * 128], V[:, tt0:tt0 + nt, c * 128:(c + 1) * 128]) for c in range(4)],
                            [("KS", t_) for t_ in range(10)] + [("V", t_) for t_ in range(tt0, tt0 + nt)], [pok])
                        stt(XT[:, ft, t0:t0 + n], po[:, :n], GTc(ft, cj), XT[:, ft, t0:t0 + n], ALU.mult, ALU.add,
                            [pok, ("MOD", cur["l"]), ("XT", ft, bi)], [("XT", ft, bi)])
                        if h == H - 1 and ft == 7:
                            next_rms(bi)

        ADA0 = AR[:, 0:8 * 3072].rearrange("p (k n) -> p k n", k=8)
        av0 = adaw_d[0].rearrange("(kt p) n -> p kt n", p=128)
        for q in range(6):
            dma("pool", ADA0[:, :, q * 512:(q + 1) * 512], av0[:, :, q * 512:(q + 1) * 512], f"ada0_{q}", writes=[("ADA0", q)])
        def mod0_group(ot):
            pm, pmk = psn()
            mmg(pm[:, 0:2], [(ADA0[:, kt, ot * 128:(ot + 1) * 128], SCT[:, kt, :]) for kt in range(8)],
                reads=[("ADA0", ot // 4), "SCT"], writes=[pmk])
            o_, _w = POFF["adab"]
            adab = PRM[:, o_ + ot:o_ + ot + 1].broadcast_to([128, 2])
            tt("dve", MODS[0][:, ot, :], pm[:, 0:2], adab, ALU.add, [pmk, "PRM"], [("MOD", 0)])

        for ot in range(16):
            mod0_group(ot)
        mod_finish(0)
        cur["l"] = 0
        rms_to_HT()
        for ot in range(16, 24):
            mod0_group(ot)
        s.barrier()
        for l in range(n_layers):
            cur["l"] = l
            mod_start(l + 1)
            try:
                ring["nar"] = [0, 1, 2, 3, 4, 6, 7] if l % 2 == 0 else [0, 1, 2, 3, 4]
                if l % 2 == 0:
                    conv_layer(l, l // 2)
                else:
                    ret_layer(l, l // 2)
            except _Stop:
                pass
            mod_drain(99)
            if snap["s"] is not None:
                s.barrier_from(snap["s"])
                snap["s"] = None
            else:
                s.barrier()

        with nc.Block() as block:
            s.emit(block)
    return nc


def _core_slots(c):
    if c < 2:
        return [("s", c, r) for r in range(4)] + [("p", c)]
    return [("p", 2 + (c - 2) * 5 + r) for r in range(5)]


_NC_CACHE = {}


def _run(inputs, n_layers=4):
    inp = {k: np.asarray(v) for k, v in inputs.items()}
    f32 = np.float32
    xp, xs = inp["x_prompt"].astype(f32), inp["x_sample"].astype(f32)
    cwin = np.ascontiguousarray(inp["conv_w_in"].reshape(2, 8, 128, 3, 16, 128).transpose(0, 4, 2, 3, 1, 5)).reshape(2, 16, 128, 3072)
    cwout = np.ascontiguousarray(inp["conv_w_out"].reshape(2, 16, 128, 8, 128).transpose(0, 3, 2, 1, 4)).reshape(2, 8, 128, 2048)
    rw = inp["ret_w_in"]
    rwqk = np.ascontiguousarray(rw[:, :, 0:2048].reshape(2, 8, 128, 16, 128).transpose(0, 3, 2, 1, 4)).reshape(2, 16, 128, 1024)
    rwvg = np.ascontiguousarray(rw[:, :, 2048:].reshape(2, 8, 128, 8, 512).transpose(0, 3, 2, 1, 4)).reshape(2, 8, 128, 4096)
    rwout = np.ascontiguousarray(inp["ret_w_out"].reshape(2, 4, 4, 128, 1024).transpose(0, 1, 3, 2, 4)).reshape(2, 4, 128, 4096)
    adaw = np.ascontiguousarray(inp["ada_w"].astype(f32))
    in_maps = []
    for c in range(8):
        slots = _core_slots(c)
        segs = []
        for sl in slots:
            if sl[0] == "s":
                segs.append(xs[sl[1], sl[2] * SEG:(sl[2] + 1) * SEG])
            else:
                segs.append(xp[sl[1]])
        x = np.concatenate(segs, axis=0)
        is_s = c < 2
        condA = inp["c"][c] if is_s else inp["c_ctx"]
        condB = inp["c_ctx"]
        cond = np.stack([condA, condB], axis=-1).reshape(8, 128, 2).transpose(1, 0, 2).reshape(128, 16)
        ext = inp["state_ret"][c if is_s else 0]
        in_maps.append({
            "xT": np.ascontiguousarray(x.T.astype(f32)),
            "cond": np.ascontiguousarray(cond.astype(f32)),
            "prm": _pack_params(inp, 1.0 if is_s else 0.0),
            "rope": _rope_tables(is_s),
            "ext": np.ascontiguousarray(ext.astype(f32)),
            "adaw": adaw, "cwin": cwin, "cwout": cwout, "rwqk": rwqk, "rwvg": rwvg, "rwout": rwout,
        })
    ck = (n_layers, _STOP)
    if ck not in _NC_CACHE:
        _NC_CACHE[ck] = build_program(n_layers)
    nc = _NC_CACHE[ck]
    res = run_bass_kernel_spmd(nc, in_maps, core_ids=list(range(8)))
    y_prompt = np.zeros((32, 256, D), f32)
    y_sample = np.zeros((2, 1024, D), f32)
    new_state = np.zeros((32, 2, 2, 4, 256, 512), f32)
    for c in range(8):
        y = res.results[c]["yT"].T
        stc = res.results[c]["st"]
        for si, sl in enumerate(_core_slots(c)):
            blk = y[si * SEG:(si + 1) * SEG]
            if sl[0] == "s":
                y_sample[sl[1], sl[2] * SEG:(sl[2] + 1) * SEG] = blk
            else:
                y_prompt[sl[1]] = blk
                new_state[sl[1]] = stc[si]
    return y_prompt, y_sample, new_state


def kernel(**inputs):
    return _run(inputs, 4)
```

```python
import numpy as np
from contextlib import ExitStack
import concourse.bass as bass
import concourse.mybir as mybir
from concourse.bass_utils import run_bass_kernel_spmd

F32 = mybir.dt.float32
BF16 = mybir.dt.bfloat16
AF = mybir.ActivationFunctionType
ALU = mybir.AluOpType

D = 1024
DI = 2048
T = 1280
SEG = 256
NSEG = 5
PAD = 15
KW = 31
H = 4
EPS = 1e-6
BLKS = [(0, 512), (512, 512), (1024, 256)]
SEGP = SEG + 2 * PAD
NDVE = 6


class Sched:
    ENG = ("pe", "act", "dve", "pool", "sp")

    def __init__(self, nc):
        self.nc = nc
        self.ops = {e: [] for e in self.ENG}
        self.sem = {e: nc.alloc_semaphore("s_" + e) for e in self.ENG}
        self.cnt = {e: 0 for e in self.ENG}
        self.dsem = {}
        self.dcnt = {}
        self.seen = {e: {} for e in self.ENG}
        self.last_w = {}
        self.reads = {}
        self.out_events = []
        self.nwaits = 0

    def _waits_for(self, eng, reads, writes):
        need = {}

        def add(ev):
            k, v = ev
            if need.get(k, 0) < v:
                need[k] = v

        for r in reads:
            if r in self.last_w:
                add(self.last_w[r])
        for w in writes:
            if w in self.last_w:
                add(self.last_w[w])
            for ev in self.reads.get(w, ()):
                add(ev)
        res = []
        seen = self.seen[eng]
        for k, v in need.items():
            if seen.get(k, 0) >= v:
                continue
            seen[k] = v
            res.append((k, v))
        self.nwaits += len(res)
        return res

    def _semobj(self, k):
        return self.sem[k] if k in self.sem else self.dsem[k]

    def _record(self, e, reads, writes):
        for r in reads:
            self.reads.setdefault(r, []).append(e)
        for w in writes:
            self.last_w[w] = e
            self.reads[w] = []

    def op(self, eng, fn, reads=(), writes=(), signal=True):
        pk = [r for r in reads if isinstance(r, tuple) and r[0] in ("ps", "psb")]
        if pk:
            writes = list(writes) + [r for r in pk if r not in writes]
        waits = self._waits_for(eng, reads, writes)
        if signal:
            self.cnt[eng] += 1
            self.ops[eng].append((waits, fn, (eng, 1)))
            self._record((eng, self.cnt[eng]), reads, writes)
        else:
            self.ops[eng].append((waits, fn, None))

    def dma(self, q, fn, semkey, reads=(), writes=(), is_output=False):
        if semkey not in self.dsem:
            self.dsem[semkey] = self.nc.alloc_semaphore("d_" + semkey)
            self.dcnt[semkey] = 0
        waits = self._waits_for(q, reads, writes)
        self.dcnt[semkey] += 16
        self.ops[q].append((waits, fn, (semkey, 16)))
        e = (semkey, self.dcnt[semkey])
        self._record(e, reads, writes)
        if is_output:
            self.out_events.append(e)

    def barrier(self):
        for e in ("pe", "act", "dve", "pool"):
            waits = []
            for o in ("pe", "act", "dve", "pool"):
                if o == e or self.cnt[o] == 0:
                    continue
                if self.seen[e].get(o, 0) < self.cnt[o]:
                    self.seen[e][o] = self.cnt[o]
                    waits.append((o, self.cnt[o]))
            if waits:
                self.ops[e].append((waits, None, None))

    def snapshot(self):
        return dict(self.cnt)

    def barrier_from(self, snap):
        comp = ("pe", "act", "dve", "pool")
        for e in comp:
            waits = []
            for o in comp:
                if o == e:
                    continue
                target = self.cnt[o] if o == "pe" else snap[o]
                if target > 0 and self.seen[e].get(o, 0) < target:
                    self.seen[e][o] = target
                    waits.append((o, target))
            if waits:
                self.ops[e].append((waits, None, None))

    def emit(self, block):
        S = self
        need = {}
        for (k, v) in self.out_events:
            need[k] = max(need.get(k, 0), v)
        final_waits = list(need.items())

        def run(engname, eng):
            for (waits, fn, inc) in S.ops[engname]:
                for (k, v) in waits:
                    eng.wait_ge(S._semobj(k), v)
                if fn is None:
                    continue
                ins = fn(eng)
                if inc is not None:
                    ins.then_inc(S._semobj(inc[0]), inc[1])
            if engname == "sp":
                for (k, v) in final_waits:
                    eng.wait_ge(S._semobj(k), v)

        @block.tensor
        def _(e):
            run("pe", e)

        @block.scalar
        def _(e):
            run("act", e)

        @block.vector
        def _(e):
            run("dve", e)

        @block.gpsimd
        def _(e):
            run("pool", e)

        @block.sync
        def _(e):
            run("sp", e)


def _param_layout():
    off = {}
    n = 0
    for name, w in [("adab", 96), ("ng", 32), ("fng", 8), ("dw", 2 * 16 * 31), ("dwb", 32), ("lng", 32),
                    ("lnb", 32), ("gng", 32), ("ld", 16), ("isS", 1), ("c127mp", 1), ("c255mp", 1), ("cp", 1),
                    ("c128pp", 1), ("dpos", 128), ("dneg", 128), ("mge", 128), ("mle", 128), ("rowi1", 128),
                    ("row128mi", 128), ("ident", 128)]:
        off[name] = (n, w)
        n += w
    return off, n


POFF, NPRM = _param_layout()


def _pack_params(inp, isS):
    P = np.zeros((128, NPRM), np.float32)

    def put(name, arr):
        o, w = POFF[name]
        P[:, o:o + w] = np.asarray(arr, np.float32).reshape(128, w)

    p = np.arange(128, dtype=np.float32)
    put("adab", inp["ada_b"].reshape(4, 24, 128).transpose(2, 0, 1))
    put("ng", inp["norm_g"].reshape(4, 8, 128).transpose(2, 0, 1))
    put("fng", inp["final_norm_g"].reshape(8, 128).transpose(1, 0))
    put("dw", inp["conv_dw"].reshape(2, 31, 16, 128).transpose(3, 0, 2, 1))
    put("dwb", inp["conv_dw_b"].reshape(2, 16, 128).transpose(2, 0, 1))
    put("lng", inp["conv_ln_g"].reshape(2, 16, 128).transpose(2, 0, 1))
    put("lnb", inp["conv_ln_b"].reshape(2, 16, 128).transpose(2, 0, 1))
    put("gng", inp["ret_gn_g"].reshape(2, 16, 128).transpose(2, 0, 1))
    put("ld", np.broadcast_to(inp["ret_log_decay"].reshape(1, 16), (128, 16)))
    put("isS", np.full((128, 1), isS))
    put("c127mp", 127 - p)
    put("c255mp", 255 - p)
    put("cp", p)
    put("c128pp", 128 + p)
    jj = p[:, None]
    ii = p[None, :]
    put("dpos", np.maximum(ii - jj, 0))
    put("dneg", np.maximum(jj - ii, 0))
    put("mge", (ii >= jj))
    put("mle", (jj >= ii))
    put("rowi1", np.broadcast_to(ii + 1, (128, 128)))
    put("row128mi", np.broadcast_to(128 - ii, (128, 128)))
    put("ident", np.eye(128))
    return P


def _rope_tables(is_sample):
    tab = np.zeros((128, 2, 1024), np.float32)
    if not is_sample:
        tab[:, 0, :] = 1.0
        return tab
    L = 1024
    GW = 64
    rows = L // GW
    r = np.repeat(np.arange(rows, dtype=np.float32), GW)
    col = np.tile(np.arange(GW, dtype=np.float32), rows)
    nf = 64
    inv = (np.float32(10000.0) ** (-np.arange(nf, dtype=np.float32) / np.float32(nf))).astype(np.float32)
    ang = np.concatenate([r[:, None] * inv, col[:, None] * inv], axis=-1).astype(np.float32)
    tab[:, 0, :] = np.cos(ang).T
    tab[:, 1, :] = np.sin(ang).T
    return tab


class _Stop(Exception):
    pass


_STOP = 99


def build_program(n_layers=4):
    nc = bass.Bass("TRN2", target_bir_lowering=False)

    def stage(k):
        if _STOP <= k:
            raise _Stop()

    def din(name, shape):
        return nc.dram_tensor(name, shape, F32, kind="ExternalInput").ap()

    xT_d = din("xT", [D, T])
    cond_d = din("cond", [128, 16])
    prm_d = din("prm", [128, NPRM])
    rope_d = din("rope", [128, 2, 1024])
    ext_d = din("ext", [2, 2, 4, 256, 512])
    adaw_d = din("adaw", [4, D, 3 * D])
    cwin_d = din("cwin", [2, 16, 128, 3072])
    cwout_d = din("cwout", [2, 8, 128, 2048])
    rwqk_d = din("rwqk", [2, 16, 128, 1024])
    rwvg_d = din("rwvg", [2, 8, 128, 4096])
    rwout_d = din("rwout", [2, 4, 128, 4096])
    yT_d = nc.dram_tensor("yT", [D, T], F32, kind="ExternalOutput").ap()
    st_d = nc.dram_tensor("st", [NSEG, 2, 2, 4, 256, 512], F32, kind="ExternalOutput").ap()

    with ExitStack() as es:
        def sb(name, shape, dt):
            return es.enter_context(nc.sbuf_tensor(name, shape, dt))

        XT = sb("XT", [128, 8, T], F32)
        HT = sb("HT", [128, 8, T], BF16)
        PRM = sb("PRM", [128, NPRM], F32)
        CONDT = sb("CONDT", [128, 16], F32)
        SCT = sb("SCT", [128, 8, 2], BF16)
        IDB = sb("IDB", [128, 128], BF16)
        ONESD = sb("ONESD", [128, 128], BF16)
        ONESI = sb("ONESI", [128, 128], BF16)
        MODS = [sb(f"MOD{i}", [128, 24, 2], F32) for i in range(4)]
        GMS = [sb(f"GM{i}", [128, 8, 2], F32) for i in range(4)]
        cur = {"l": 0}
        pending = []
        WA = [sb(f"WA{i}", [128, 2, 8, 128], BF16) for i in range(2)]
        WB = [sb(f"WB{i}", [128, 4096], BF16) for i in range(2)]
        ADA = [sb(f"ADA{i}", [128, 8, 128], BF16) for i in range(2)]
        RSB = sb("RSB", [128, 512], F32)
        TF = [sb(f"TF{i}", [128, 512], F32) for i in range(4)]
        TB = [sb(f"TB{i}", [128, 512], BF16) for i in range(4)]
        FST = [sb(f"FST{i}", [128, 2, 512], F32) for i in range(2)]
        SMALL = sb("SMALL", [128, 64], F32)
        LG = sb("LG", [128, 8], F32)
        COLS = sb("COLS", [128, 4, 8], F32)
        ARF = sb("ARF", [128, 3584], F32)
        ARN = 37376
        AR = sb("AR", [128, ARN], BF16)
        ps = [es.enter_context(nc.psum_tensor(f"ps{i}", [128, 512], F32)) for i in range(8)]
        psbs = [ps[6][:].bitcast(BF16), ps[7][:].bitcast(BF16)]
        PSTAT = 5

        s = Sched(nc)

        def P_(name, a=0, w=None):
            o, ww = POFF[name]
            if w is None:
                w = ww - a
            return PRM[:, o + a:o + a + w]

        def act(out, in_, func, reads, writes, scale=None, bias=None):
            kw = {}
            if scale is not None:
                kw["scale"] = scale
            if bias is not None:
                kw["bias"] = bias
            s.op("act", lambda e: e.activation(out=out, in_=in_, func=func, **kw), reads, writes)

        def tt(eng, out, in0, in1, op, reads, writes):
            s.op(eng, lambda e: e.tensor_tensor(out=out, in0=in0, in1=in1, op=op), reads, writes)

        def ts(eng, out, in0, s1, op0, reads, writes, s2=None, op1=None):
            if op1 is None:
                s.op(eng, lambda e: e.tensor_scalar(out=out, in0=in0, scalar1=s1, scalar2=None, op0=op0), reads, writes)
            else:
                s.op(eng, lambda e: e.tensor_scalar(out=out, in0=in0, scalar1=s1, scalar2=s2, op0=op0, op1=op1), reads, writes)

        def stt(out, in0, scalar, in1, op0, op1, reads, writes):
            s.op("dve", lambda e: e.scalar_tensor_tensor(out=out, in0=in0, scalar=scalar, in1=in1, op0=op0, op1=op1), reads, writes)

        def mmg(out, pairs, reads, writes, transpose=False):
            n = len(pairs)
            for i, (l, r) in enumerate(pairs):
                s.op("pe", (lambda e, l=l, r=r, i=i: e.matmul(out, lhsT=l, rhs=r, start=(i == 0), stop=(i == n - 1))),
                     reads, writes, signal=(i == n - 1))

        def tr(out, in_, reads, writes, signal=True):
            s.op("pe", lambda e: e.transpose(out=out, in_=in_, identity=IDB[:]), reads, writes, signal=signal)

        def cpy(eng, out, in_, reads, writes):
            if eng == "act":
                act(out, in_, AF.Copy, reads, writes)
            else:
                s.op(eng, lambda e: e.tensor_copy(out=out, in_=in_), reads, writes)

        def dma(q, out, in_, semkey, reads=(), writes=(), is_output=False, mdl=None):
            if mdl is None:
                s.dma(q, lambda e: e.dma_start(out=out, in_=in_), semkey, reads, writes, is_output)
            else:
                s.dma(q, lambda e: e.dma_start(out=out, in_=in_, max_dma_last_dim=mdl), semkey, reads, writes, is_output)

        ring = {"tf": 0, "tb": 0, "ps": 0, "psw": 0, "wide": False, "tfn": 4, "wn": 8, "nar": [0, 1, 2, 3, 4]}

        def tf():
            i = ring["tf"]
            i = i % ring["tfn"]
            ring["tf"] = (i + 1) % ring["tfn"]
            return TF[i], ("tf", i)

        def tb():
            i = ring["tb"]
            ring["tb"] = (i + 1) % 4
            return TB[i], ("tb", i)

        def psn():
            if ring["wide"]:
                i = ring["psw"]
                i = i % ring["wn"]
                ring["psw"] = (i + 1) % ring["wn"]
                return ps[i], ("ps", i)
            nar = ring["nar"]
            i = ring["ps"] % len(nar)
            ring["ps"] = (i + 1) % len(nar)
            return ps[nar[i]], ("ps", nar[i])

        dma("sp", PRM[:], prm_d[:], "prm", writes=["PRM"])
        dma("sp", CONDT[:], cond_d[:], "cond", writes=["CONDT"])
        xT_v = xT_d.rearrange("(kt p) t -> p kt t", p=128)
        for kt in range(8):
            dma("sp", XT[:, kt, :], xT_v[:, kt, :], f"x{kt}", writes=[("XT", kt, b) for b in range(3)])
        EPSC = SMALL[:, 63:64]
        s.op("pool", lambda e: e.memset(EPSC, EPS), writes=["EPSC"])
        s.op("pool", lambda e: e.memset(ONESD[:], 1.0 / D), writes=["ONESD"])
        s.op("pool", lambda e: e.memset(ONESI[:], 1.0 / DI), writes=["ONESI"])
        cpy("dve", IDB[:], P_("ident"), ["PRM"], ["IDB"])
        act(SCT[:].rearrange("p k j -> p (k j)"), CONDT[:], AF.Silu, ["CONDT"], ["SCT"])

        def mod_group(l, ot):
            av = adaw_d[l].rearrange("(kt p) n -> p kt n", p=128)
            sl = ot % 2
            dma("pool", ADA[sl][:], av[:, :, ot * 128:(ot + 1) * 128], f"ada{sl}", writes=[("ADA", sl)])
            pm, pmk = psn()
            mmg(pm[:, 0:2], [(ADA[sl][:, kt, :], SCT[:, kt, :]) for kt in range(8)],
                reads=[("ADA", sl), "SCT"], writes=[pmk])
            o_, _w = POFF["adab"]
            adab = PRM[:, o_ + l * 24 + ot:o_ + l * 24 + ot + 1].broadcast_to([128, 2])
            tt("dve", MODS[l][:, ot, :], pm[:, 0:2], adab, ALU.add, [pmk, "PRM"], [("MOD", l)])

        def mod_finish(l):
            o_, _w = POFF["ng"]
            ngb = PRM[:, o_ + l * 8:o_ + (l + 1) * 8].unsqueeze(2).broadcast_to([128, 8, 2])
            ts("dve", GMS[l][:], MODS[l][:, 8:16, :], 1.0, ALU.add, [("MOD", l)], [("GM", l)])
            tt("dve", GMS[l][:], GMS[l][:], ngb, ALU.mult, [("GM", l), "PRM"], [("GM", l)])

        def mod_start(l):
            if l < n_layers:
                pending.extend((l, ot) for ot in range(24))

        def mod_drain(k):
            while k > 0 and pending:
                l, ot = pending.pop(0)
                mod_group(l, ot)
                if ot == 23:
                    mod_finish(l)
                k -= 1

        def SHc(kt, cj):
            return MODS[cur["l"]][:, kt, cj:cj + 1]

        def GTc(kt, cj):
            return MODS[cur["l"]][:, 16 + kt, cj:cj + 1]

        def rms_block(bi, gfun, outfun):
            t0, n = BLKS[bi]
            for kt in range(8):
                sq, sqk = tb()
                act(sq[:, :n], XT[:, kt, t0:t0 + n], AF.Square, [("XT", kt, bi)], [sqk])
                s.op("pe", (lambda e, kt=kt, sq=sq: e.matmul(ps[PSTAT][:, :n], lhsT=ONESD[:], rhs=sq[:, :n], start=(kt == 0), stop=(kt == 7))),
                     [sqk, "ONESD"], [("ps", PSTAT)], signal=True)
            rs, rsk = RSB, "RSB"
            act(rs[:, :n], ps[PSTAT][:, :n], AF.Sqrt, [("ps", PSTAT), "EPSC"], [rsk], bias=EPSC, scale=1.0)
            s.op("dve", lambda e: e.reciprocal(out=rs[:, :n], in_=rs[:, :n]), [rsk], [rsk])
            for kt in range(8):
                tmp, tk = tf()
                stt(tmp[:, :n], XT[:, kt, t0:t0 + n], gfun(kt), rs[:, :n], ALU.mult, ALU.mult,
                    [("XT", kt, bi), rsk, ("GM", cur["l"]), "PRM"], [tk])
                outfun(kt, tmp, tk)

        def rms_to_HT_blk(bi):
            t0, n = BLKS[bi]
            cj = 0 if bi < 2 else 1

            def outf(kt, tmp, tk):
                act(HT[:, kt, t0:t0 + n], tmp[:, :n], AF.Identity, [tk, ("MOD", cur["l"])], [("HT", kt, bi)],
                    bias=SHc(kt, cj), scale=1.0)

            rms_block(bi, lambda kt: GMS[cur["l"]][:, kt, cj:cj + 1], outf)

        def rms_to_HT():
            arfk = [("RS", b_) for b_ in range(3)] + [("NMR", b_) for b_ in range(3)]
            RSv = [ARF[:, b_ * 512:(b_ + 1) * 512] for b_ in range(3)]
            banks = [(ps[PSTAT], ("ps", PSTAT)), psn(), psn()]
            for bi, (t0, n) in enumerate(BLKS):
                pb_, pbk_ = banks[bi]
                for kt in range(8):
                    sq, sqk = tb()
                    act(sq[:, :n], XT[:, kt, t0:t0 + n], AF.Square, [("XT", kt, bi)], [sqk])
                    s.op("pe", (lambda e, kt=kt, sq=sq, n=n, pb_=pb_: e.matmul(pb_[:, :n], lhsT=ONESD[:], rhs=sq[:, :n], start=(kt == 0), stop=(kt == 7))),
                         [sqk, "ONESD"], [pbk_], signal=True)
            for bi, (t0, n) in enumerate(BLKS):
                pb_, pbk_ = banks[bi]
                rs = RSv[bi]
                act(rs[:, :n], pb_[:, :n], AF.Sqrt, [pbk_, "EPSC"], arfk, bias=EPSC, scale=1.0)
                s.op("dve", (lambda e, rs=rs, n=n: e.reciprocal(out=rs[:, :n], in_=rs[:, :n])), arfk, arfk)
            for bi, (t0, n) in enumerate(BLKS):
                cj = 0 if bi < 2 else 1
                rs = RSv[bi]
                for kt in range(8):
                    tmp, tk = tf()
                    stt(tmp[:, :n], XT[:, kt, t0:t0 + n], GMS[cur["l"]][:, kt, cj:cj + 1], rs[:, :n], ALU.mult, ALU.mult,
                        [("XT", kt, bi), ("GM", cur["l"]), "PRM"] + arfk, [tk])
                    act(HT[:, kt, t0:t0 + n], tmp[:, :n], AF.Identity, [tk, ("MOD", cur["l"])], [("HT", kt, bi)],
                        bias=SHc(kt, cj), scale=1.0)

        def final_norm_blk(bi):
            yv = yT_d.rearrange("(kt p) t -> p kt t", p=128)
            o_, _w = POFF["fng"]
            t0, n = BLKS[bi]

            def outf(kt, tmp, tk):
                dma("sp", yv[:, kt, t0:t0 + n], tmp[:, :n], f"y{tk[1]}", reads=[tk], is_output=True)

            rms_block(bi, lambda kt: PRM[:, o_ + kt:o_ + kt + 1], outf)

        def next_rms(bi):
            l = cur["l"]
            if l + 1 < n_layers:
                assert not [p_ for p_ in pending if p_[0] == l + 1]
                cur["l"] = l + 1
                rms_to_HT_blk(bi)
                cur["l"] = l
            else:
                final_norm_blk(bi)

        prep_done = set()
        snap = {"s": None}

        def qk_dma(j, h):
            for which in range(2):
                for dkt in range(2):
                    ot = which * 8 + h * 2 + dkt
                    if which == 0:
                        dma("pool", WA[dkt][:, 0].rearrange("p k m -> p (k m)"), rwqk_d[j, ot], f"wA{dkt}",
                            writes=[("wA", dkt)], mdl=4096)
                    else:
                        dma("pool", ADA[dkt][:].rearrange("p k m -> p (k m)"), rwqk_d[j, ot], f"ada{dkt}",
                            writes=[("ADA", dkt)], mdl=4096)

        def ret_prep(j):
            prep_done.add(j)
            COS = ARF[:, 0:1024]
            SIN = ARF[:, 1024:2048]
            DT = ARF[:, 2048:2560].rearrange("p (h i) -> p h i", h=4)
            ARF16 = ARF[:].bitcast(BF16)
            WQF = ARF16[:, 5120:5632].rearrange("p (h i) -> p h i", h=4)
            WQB = ARF16[:, 5632:6144].rearrange("p (h i) -> p h i", h=4)
            isS = P_("isS")
            qk_dma(j, 0)
            dma("sp", ARF[:, 0:2048], rope_d.rearrange("p a t -> p (a t)"), "rope",
                writes=["rope"] + [("RS", b) for b in range(3)] + [("NMR", b) for b in range(3)])
            old, _w = POFF["ld"]
            act(LG[:], PRM[:, old + j * 8:old + (j + 1) * 8], AF.Exp, ["PRM"], ["LG"])
            ts("dve", LG[:], LG[:], -1.0, ALU.mult, ["LG"], ["LG"])
            for h in range(H):
                lgf = LG[:, h:h + 1]
                lgb = LG[:, 4 + h:5 + h]
                e1, e1k = tf()
                e2, e2k = tf()
                act(e1[:, 0:128], P_("dpos"), AF.Exp, ["PRM", "LG"], [e1k], scale=lgf)
                tt("dve", e1[:, 0:128], e1[:, 0:128], P_("mge"), ALU.mult, [e1k, "PRM"], [e1k])
                act(e2[:, 0:128], P_("dneg"), AF.Exp, ["PRM", "LG"], [e2k], scale=lgb)
                tt("dve", e2[:, 0:128], e2[:, 0:128], P_("mle"), ALU.mult, [e2k, "PRM"], [e2k])
                tt("dve", DT[:, h, :], e1[:, 0:128], e2[:, 0:128], ALU.add, [e1k, e2k], ["DT"] + [("NMR", b_) for b_ in range(3)])
                act(WQF[:, h, :], P_("rowi1"), AF.Exp, ["PRM", "LG"], ["WQ"], scale=lgf)
                act(WQB[:, h, :], P_("row128mi"), AF.Exp, ["PRM", "LG"], ["WQ"], scale=lgb)
                act(COLS[:, h, 0:1], P_("c127mp"), AF.Exp, ["PRM", "LG"], ["COLS"], scale=lgf)
                act(COLS[:, h, 1:2], P_("c255mp"), AF.Exp, ["PRM", "LG"], ["COLS"], scale=lgf)
                act(COLS[:, h, 2:3], P_("cp"), AF.Exp, ["PRM", "LG"], ["COLS"], scale=lgb)
                act(COLS[:, h, 3:4], P_("c128pp"), AF.Exp, ["PRM", "LG"], ["COLS"], scale=lgb)
                act(COLS[:, h, 4:5], lgf, AF.Exp, ["LG"], ["COLS"], scale=128.0)
                act(COLS[:, h, 5:6], lgf, AF.Exp, ["LG"], ["COLS"], scale=256.0)
                act(COLS[:, h, 6:7], lgb, AF.Exp, ["LG"], ["COLS"], scale=128.0)
                act(COLS[:, h, 7:8], lgb, AF.Exp, ["LG"], ["COLS"], scale=256.0)
                ts("dve", COLS[:, h, 4:8], COLS[:, h, 4:8], isS, ALU.mult, ["COLS", "PRM"], ["COLS"])

        def conv_layer(l, j):
            C = AR[:, 0:16 * T].rearrange("p (c t) -> p c t", c=16)
            DG = [AR[:, 20480 + i * 3968:20480 + (i + 1) * 3968].rearrange("p (k m) -> p k m", k=KW) for i in range(2)]
            UP = [AR[:, 28416 + i * 1430:28416 + (i + 1) * 1430].rearrange("p (s t) -> p s t", s=NSEG) for i in range(2)]
            RS = ARF[:, 0:T]
            NMR = ARF[:, T:2 * T]
            isS = P_("isS")
            for i in range(2):
                s.op("dve", (lambda e, i=i: e.memset(UP[i], 0.0)), writes=[("up", i)])
            if l == 0:
                rms_to_HT()
            odw, _w = POFF["dw"]
            odwb, _w = POFF["dwb"]
            dma("pool", WA[0][:].rearrange("p w k m -> p (w k m)"), cwin_d[j, 0, :, 0:2048], "wA0",
                writes=[("wA", 0)], mdl=4096)

            def ab(ct):
                sl = ct % 2
                if ct + 1 < 16:
                    dma("pool", WA[1 - sl][:].rearrange("p w k m -> p (w k m)"), cwin_d[j, ct + 1, :, 0:2048], f"wA{1 - sl}",
                        writes=[("wA", 1 - sl)], mdl=4096)
                dwc = PRM[:, odw + (j * 16 + ct) * KW + NDVE:odw + (j * 16 + ct + 1) * KW]
                tt("dve", DG[sl][:, NDVE:KW, :], IDB[:].unsqueeze(1).broadcast_to([128, KW - NDVE, 128]),
                   dwc.unsqueeze(2).broadcast_to([128, KW - NDVE, 128]), ALU.mult, ["IDB", "PRM"], [("dg", sl)])

            def ab_blk(ct, bi):
                sl = ct % 2
                if True:
                    t0, n = BLKS[bi]
                    ns = n // SEG
                    pa, pak = psn()
                    pb, pbk = psn()
                    hk = [("HT", kt, bi) for kt in range(8)]
                    mmg(pa[:, :n], [(WA[sl][:, 0, kt, :], HT[:, kt, t0:t0 + n]) for kt in range(8)], [("wA", sl)] + hk, [pak])
                    mmg(pb[:, :n], [(WA[sl][:, 1, kt, :], HT[:, kt, t0:t0 + n]) for kt in range(8)], [("wA", sl)] + hk, [pbk])
                    sg, sgk = tf()
                    act(sg[:, :n], pb[:, :n], AF.Sigmoid, [pbk], [sgk])
                    tt("dve", UP[sl][:, 2 * bi:2 * bi + ns, PAD:PAD + SEG],
                       pa[:, :n].rearrange("p (s t) -> p s t", s=ns), sg[:, :n].rearrange("p (s t) -> p s t", s=ns),
                       ALU.mult, [pak, sgk], [("up", sl)])

            def pads(ct):
                sl = ct % 2
                for r in range(1, 4):
                    ts("dve", UP[sl][:, r, 0:PAD], UP[sl][:, r - 1, SEG:SEG + PAD], isS, ALU.mult, [("up", sl), "PRM"], [("up", sl)])
                for r in range(0, 3):
                    ts("dve", UP[sl][:, r, PAD + SEG:SEGP], UP[sl][:, r + 1, PAD:2 * PAD], isS, ALU.mult, [("up", sl), "PRM"], [("up", sl)])

            def dwconv_blk(ct, bi):
                sl = ct % 2
                for _ in range(1):
                    t0, n = BLKS[bi]
                    ns = n // SEG
                    pc, pck = psn()
                    mmg(pc[:, :n].rearrange("p (s t) -> p s t", s=ns),
                        [(DG[sl][:, k, :], UP[sl][:, 2 * bi:2 * bi + ns, k:k + SEG]) for k in range(NDVE, KW)],
                        [("dg", sl), ("up", sl)], [pck])
                    bcol = PRM[:, odwb + j * 16 + ct:odwb + j * 16 + ct + 1]
                    if NDVE == 0:
                        act(C[:, ct, t0:t0 + n], pc[:, :n], AF.Identity, [pck, "PRM"], [("C", ct, bi)], bias=bcol, scale=1.0)
                        continue
                    acc, acck = tf()
                    accv = acc[:, :n].rearrange("p (s t) -> p s t", s=ns)
                    for k in range(NDVE):
                        dcol = PRM[:, odw + (j * 16 + ct) * KW + k:odw + (j * 16 + ct) * KW + k + 1]
                        uv = UP[sl][:, 2 * bi:2 * bi + ns, k:k + SEG]
                        if k == 0:
                            ts("dve", accv, uv, dcol, ALU.mult, [("up", sl), "PRM"], [acck])
                        else:
                            stt(accv, uv, dcol, accv, ALU.mult, ALU.add, [("up", sl), "PRM", acck], [acck])
                    stt(C[:, ct, t0:t0 + n], pc[:, :n], bcol, acc[:, :n], ALU.add, ALU.add, [pck, "PRM", acck], [("C", ct, bi)])

            ab(0)
            for bi in range(3):
                ab_blk(0, bi)
            pads(0)
            for ct in range(16):
                if ct + 1 < 16:
                    ab(ct + 1)
                for bi in range(3):
                    if ct + 1 < 16:
                        ab_blk(ct + 1, bi)
                        if bi == 2:
                            pads(ct + 1)
                    dwconv_blk(ct, bi)
                    if bi < 2:
                        mod_drain(1)
            sbank = [psn() for _ in range(3)]
            qbank = [psn() for _ in range(3)]
            for bi, (t0, n) in enumerate(BLKS):
                p1, p1k = sbank[bi]
                p2, p2k = qbank[bi]
                for ct in range(16):
                    s.op("pe", (lambda e, ct=ct, t0=t0, n=n, p1=p1: e.matmul(p1[:, :n], lhsT=ONESI[:], rhs=C[:, ct, t0:t0 + n], start=(ct == 0), stop=(ct == 15))),
                         [("C", ct, bi), "ONESI"], [p1k], signal=(ct == 15))
                for ct in range(16):
                    sq, sqk = tb()
                    if ct % 3 == 2:
                        tt("dve", sq[:, :n], C[:, ct, t0:t0 + n], C[:, ct, t0:t0 + n], ALU.mult, [("C", ct, bi)], [sqk])
                    else:
                        act(sq[:, :n], C[:, ct, t0:t0 + n], AF.Square, [("C", ct, bi)], [sqk])
                    s.op("pe", (lambda e, ct=ct, sq=sq, n=n, p2=p2: e.matmul(p2[:, :n], lhsT=ONESI[:], rhs=sq[:, :n], start=(ct == 0), stop=(ct == 15))),
                         [sqk, "ONESI"], [p2k], signal=True)
            for bi, (t0, n) in enumerate(BLKS):
                p1, p1k = sbank[bi]
                p2, p2k = qbank[bi]
                mu, muk = tf()
                cpy("dve", mu[:, :n], p1[:, :n], [p1k], [muk])
                m2, m2k = tf()
                tt("dve", m2[:, :n], mu[:, :n], mu[:, :n], ALU.mult, [muk], [m2k])
                tt("dve", m2[:, :n], p2[:, :n], m2[:, :n], ALU.subtract, [p2k, m2k], [m2k])
                ts("dve", m2[:, :n], m2[:, :n], 0.0, ALU.max, [m2k], [m2k], s2=EPS, op1=ALU.add)
                act(m2[:, :n], m2[:, :n], AF.Sqrt, [m2k], [m2k])
                s.op("dve", (lambda e, m2=m2, t0=t0, n=n: e.reciprocal(out=RS[:, t0:t0 + n], in_=m2[:, :n])), [m2k], [("RS", bi)])
                stt(NMR[:, t0:t0 + n], mu[:, :n], -1.0, RS[:, t0:t0 + n], ALU.mult, ALU.mult, [muk, ("RS", bi)], [("NMR", bi)])
            WT = [(WB[0][:, 0:2048], ("wB", 0)), (WB[0][:, 2048:4096], ("wB", 0)),
                  (WB[1][:, 0:2048], ("wB", 1)), (WB[1][:, 2048:4096], ("wB", 1)),
                  (AR[:, 31276:31276 + 2048], ("ARs", 0)), (AR[:, 31276 + 2048:31276 + 4096], ("ARs", 1)),
                  (AR[:, 20480:20480 + 2048], ("dg", 0)), (AR[:, 20480 + 3968:20480 + 3968 + 2048], ("dg", 1))]
            for ft in range(8):
                dma("pool", WT[ft][0], cwout_d[j, ft], f"wt{ft}", writes=[WT[ft][1]], mdl=4096)
            olng, _w = POFF["lng"]
            olnb, _w = POFF["lnb"]
            dma("pool", WA[0][:, 0].rearrange("p k m -> p (k m)"), cwin_d[j, 0, :, 2048:3072], "wA0",
                writes=[("wA", 0)], mdl=4096)
            pend_mult = []
            for ct in range(16):
                sl = ct % 2
                if ct + 1 < 16:
                    dma("pool", WA[1 - sl][:, 0].rearrange("p k m -> p (k m)"), cwin_d[j, ct + 1, :, 2048:3072], f"wA{1 - sl}",
                        writes=[("wA", 1 - sl)], mdl=4096)
                for bi, (t0, n) in enumerate(BLKS):
                    pg, pgk = psn()
                    hk = [("HT", kt, bi) for kt in range(8)]
                    mmg(pg[:, :n], [(WA[sl][:, 0, kt, :], HT[:, kt, t0:t0 + n]) for kt in range(8)], [("wA", sl)] + hk, [pgk])
                    sgb, sgbk = tb()
                    act(sgb[:, :n], pg[:, :n], AF.Silu, [pgk], [sgbk])
                    t1, t1k = tf()
                    tt("dve", t1[:, :n], C[:, ct, t0:t0 + n], RS[:, t0:t0 + n], ALU.mult, [("C", ct, bi), ("RS", bi)], [t1k])
                    tt("dve", t1[:, :n], t1[:, :n], NMR[:, t0:t0 + n], ALU.add, [t1k, ("NMR", bi)], [t1k])
                    yb, ybk = tb()
                    act(yb[:, :n], t1[:, :n], AF.Silu, [t1k, "PRM"], [ybk],
                        scale=PRM[:, olng + j * 16 + ct:olng + j * 16 + ct + 1],
                        bias=PRM[:, olnb + j * 16 + ct:olnb + j * 16 + ct + 1])
                    pend_mult.append((C[:, ct, t0:t0 + n], yb[:, :n], sgb[:, :n], [ybk, sgbk], [("C", ct, bi)]))
                    if len(pend_mult) > 1:
                        o_, a_, b_, r_, w_ = pend_mult.pop(0)
                        tt("dve", o_, a_, b_, ALU.mult, r_, w_)
            o_, a_, b_, r_, w_ = pend_mult.pop(0)
            tt("dve", o_, a_, b_, ALU.mult, r_, w_)
            if l + 1 < n_layers and (l + 1) // 2 not in prep_done:
                ret_prep((l + 1) // 2)
            mod_drain(99)
            snap["s"] = s.snapshot()
            for bi, (t0, n) in enumerate(BLKS):
                cj = 0 if bi < 2 else 1
                for ft in range(8):
                    wt, wtk = WT[ft]
                    wv = wt.rearrange("p (c m) -> p c m", c=16)
                    po, pok = psn()
                    mmg(po[:, :n], [(wv[:, c, :], C[:, c, t0:t0 + n]) for c in range(16)],
                        [wtk] + [("C", c, bi) for c in range(16)], [pok])
                    stt(XT[:, ft, t0:t0 + n], po[:, :n], GTc(ft, cj), XT[:, ft, t0:t0 + n], ALU.mult, ALU.add,
                        [pok, ("MOD", cur["l"]), ("XT", ft, bi)], [("XT", ft, bi)])
                next_rms(bi)

        def ret_layer(l, j):
            R1 = AR[:, 0:4 * T].rearrange("p (r t) -> p r t", r=4)
            V = AR[:, 5120:10240].rearrange("p (t v) -> p t v", t=10)
            SG = AR[:, 10240:15360].rearrange("p (t v) -> p t v", t=10)
            KS = AR[:, 15360:23040].rearrange("p (t w k) -> p t w k", t=10, w=3)
            BST = AR[:, 23040:32256].rearrange("p (r d v) -> p r d v", r=9, d=2)
            FBF = AR[:, 32256:36352].rearrange("p (r d v) -> p r d v", r=4, d=2)
            QFS = AR[:, 36352:36864].rearrange("p (d t) -> p d t", d=2)
            QBS = AR[:, 36864:37376].rearrange("p (d t) -> p d t", d=2)
            COS = ARF[:, 0:1024]
            SIN = ARF[:, 1024:2048]
            DT = ARF[:, 2048:2560].rearrange("p (h i) -> p h i", h=4)
            ARF16 = ARF[:].bitcast(BF16)
            WQF = ARF16[:, 5120:5632].rearrange("p (h i) -> p h i", h=4)
            WQB = ARF16[:, 5632:6144].rearrange("p (h i) -> p h i", h=4)
            isS = P_("isS")
            if j not in prep_done:
                ret_prep(j)
            stage(1)
            if l == 0:
                rms_to_HT()
            stage(2)
            ogn, _w = POFF["gng"]

            def a1_dma(h):
                qk_dma(j, h)

            def a1(h):
                dma("sp", FST[1][:], ext_d[j, 1, h].rearrange("(t p) v -> p t v", p=128), "extb",
                    writes=[("FST", 1, 0), ("FST", 1, 1)])
                dma("sp", FST[0][:], ext_d[j, 0, h].rearrange("(t p) v -> p t v", p=128), "extf",
                    writes=[("FST", 0, 0), ("FST", 0, 1)])
                for which in range(2):
                    kscale = 1.0 if which == 0 else 0.0625
                    for bi, (t0, n) in enumerate(BLKS):
                        hk = [("HT", kt, bi) for kt in range(8)]
                        p1, p1k = psn()
                        p2, p2k = psn()
                        if which == 0:
                            w1, w2, wk1, wk2 = WA[0][:, 0], WA[1][:, 0], ("wA", 0), ("wA", 1)
                        else:
                            w1, w2, wk1, wk2 = ADA[0], ADA[1], ("ADA", 0), ("ADA", 1)
                        mmg(p1[:, :n], [(w1[:, kt, :], HT[:, kt, t0:t0 + n]) for kt in range(8)], [wk1] + hk, [p1k])
                        mmg(p2[:, :n], [(w2[:, kt, :], HT[:, kt, t0:t0 + n]) for kt in range(8)], [wk2] + hk, [p2k])
                        d1 = R1[:, which * 2 + 0, t0:t0 + n]
                        d2 = R1[:, which * 2 + 1, t0:t0 + n]
                        segs = [2 * bi, 2 * bi + 1] if bi < 2 else [4]
                        k1s = [("R1", which * 2 + 0, sg_) for sg_ in segs]
                        k2s = [("R1", which * 2 + 1, sg_) for sg_ in segs]
                        if bi < 2:
                            a_, ak = tf()
                            b_, bk = tf()
                            stt(a_[:, :n], p1[:, :n], kscale, COS[:, t0:t0 + n], ALU.mult, ALU.mult, [p1k, "rope"], [ak])
                            stt(b_[:, :n], p2[:, :n], kscale, SIN[:, t0:t0 + n], ALU.mult, ALU.mult, [p2k, "rope"], [bk])
                            tt("dve", d1, a_[:, :n], b_[:, :n], ALU.subtract, [ak, bk], k1s)
                            c_, ck = tf()
                            d_, dk_ = tf()
                            stt(c_[:, :n], p2[:, :n], kscale, COS[:, t0:t0 + n], ALU.mult, ALU.mult, [p2k, "rope"], [ck])
                            stt(d_[:, :n], p1[:, :n], kscale, SIN[:, t0:t0 + n], ALU.mult, ALU.mult, [p1k, "rope"], [dk_])
                            tt("dve", d2, c_[:, :n], d_[:, :n], ALU.add, [ck, dk_], k2s)
                        else:
                            act(d1, p1[:, :n], AF.Copy, [p1k], k1s, scale=kscale)
                            act(d2, p2[:, :n], AF.Copy, [p2k], k2s, scale=kscale)

            def wb_dma(h):
                for which in range(2):
                    dma("pool", WB[which][:], rwvg_d[j, which * 4 + h], f"wB{which}", writes=[("wB", which)], mdl=4096)

            a1(0)
            for h in range(H):
                stage(3)
                if h == 0:
                    wb_dma(0)

                def proj(which, tti):
                    sl = which
                    wv = WB[sl][:].rearrange("p (k n) -> p k n", k=8)
                    bi = tti // 4
                    hk = [("HT", kt, bi) for kt in range(8)]
                    pv, pvk = psn()
                    mmg(pv[:], [(HT[:, kt, tti * 128:(tti + 1) * 128], wv[:, kt, :]) for kt in range(8)], [("wB", sl)] + hk, [pvk])
                    if which == 0:
                        cpy("dve", V[:, tti, :], pv[:], [pvk], [("V", tti)])
                    else:
                        act(SG[:, tti, :], pv[:], AF.Silu, [pvk], [("SG", tti)])

                PTS = []
                for tti in range(10):
                    proj(0, tti)
                    rg = tti % 2
                    psb = psbs[rg]
                    for dkt in range(2):
                        tr(psb[:, dkt * 128:(dkt + 1) * 128], R1[:, 2 + dkt, tti * 128:(tti + 1) * 128],
                           [("R1", 2 + dkt, tti // 2), "IDB"], [("ps", 6 + rg)], signal=(dkt == 1))
                    src_ = psb[:, 0:256]
                    cols = [0, 1, 2] if tti % 2 == 0 else [0, 2, 3]
                    for w_, cidx in enumerate(cols):
                        col = COLS[:, h, cidx:cidx + 1]
                        act(KS[:, tti, w_, :], src_, AF.Identity, [("ps", 6 + rg), "COLS"], [("KS", tti)], scale=col)
                    cs = slice(tti * 128, (tti + 1) * 128)
                    psS, psSk = psn()
                    mmg(psS[:, 0:128], [(R1[:, 2 + dkt, cs], R1[:, dkt, cs]) for dkt in range(2)],
                        [("R1", i_, tti // 2) for i_ in range(4)], [psSk])
                    pt = TB[tti // 4][:, (tti % 4) * 128:(tti % 4 + 1) * 128]
                    ptk = ("tb", tti // 4)
                    tt("dve", pt, psS[:, 0:128], DT[:, h, :], ALU.mult, [psSk, "DT"], [ptk])
                    PTS.append((pt, ptk))
                    if tti < 6:
                        mod_drain(1)
                stage(5)

                def kv_group(c_first, w_first, c_second, w_second):
                    outp = []
                    for dkt in range(2):
                        pp, ppk = psn()
                        pairs = [(KS[:, c_first, w_first, dkt * 128:(dkt + 1) * 128], V[:, c_first, :])]
                        rd = [("KS", c_first), ("V", c_first)]
                        if c_second is not None:
                            pairs.append((KS[:, c_second, w_second, dkt * 128:(dkt + 1) * 128], V[:, c_second, :]))
                            rd += [("KS", c_second), ("V", c_second)]
                        mmg(pp[:], pairs, rd, [ppk])
                        outp.append((pp, ppk))
                    return outp

                ring["wide"] = True

                def st_out(r, d, FS):
                    dst = st_d[r, j, d, h].rearrange("(t p) v -> p t v", p=128)
                    for dkt in range(2):
                        dma("sp", dst[:, dkt, :], FS[:, dkt, :], f"st{d}{dkt}", reads=[("FST", d, dkt)], is_output=True)

                FF = FST[0]
                QF2 = TF[3][:].bitcast(BF16)
                QSETS = [(QFS, QBS, "Q0"), (QF2[:, 0:512].rearrange("p (d t) -> p d t", d=2),
                                            QF2[:, 512:1024].rearrange("p (d t) -> p d t", d=2), ("tf", 3))]

                def fwd_chain(r):
                    c0, c1 = 2 * r, 2 * r + 1
                    par = r % 2
                    pa = kv_group(c0, 0, None, None)
                    pbf = kv_group(c0, 1, c1, 0)
                    fin_k = ("FBF", par * 2)
                    fc1_k = ("FBF", par * 2 + 1)
                    for dkt in range(2):
                        fk = ("FST", 0, dkt)
                        if r < 4:
                            stt(FBF[:, par * 2 + 1, dkt, :], FF[:, dkt, :], COLS[:, h, 4:5], pa[dkt][0][:], ALU.mult, ALU.add,
                                [fk, "COLS", pa[dkt][1]], [fc1_k])
                        else:
                            cpy("act", FBF[:, par * 2 + 1, dkt, :], pa[dkt][0][:], [pa[dkt][1]], [fc1_k])
                    for dkt in range(2):
                        fk = ("FST", 0, dkt)
                        if r < 4:
                            stt(FF[:, dkt, :], FF[:, dkt, :], COLS[:, h, 5:6], pbf[dkt][0][:], ALU.mult, ALU.add,
                                [fk, "COLS", pbf[dkt][1]], [fk])
                        else:
                            cpy("dve", FF[:, dkt, :], pbf[dkt][0][:], [pbf[dkt][1]], [fk])
                    st_out(r, 0, FF)
                    qf, qb, qk = QSETS[par]
                    s0 = r * SEG
                    for dkt in range(2):
                        qv = R1[:, dkt, s0:s0 + SEG].rearrange("p (c i) -> p c i", c=2)
                        tt("dve", qf[:, dkt, :].rearrange("p (c i) -> p c i", c=2), qv,
                           WQF[:, h, :].unsqueeze(1).broadcast_to([128, 2, 128]), ALU.mult, [("R1", dkt, r), "WQ"], [qk])
                        tt("dve", qb[:, dkt, :].rearrange("p (c i) -> p c i", c=2), qv,
                           WQB[:, h, :].unsqueeze(1).broadcast_to([128, 2, 128]), ALU.mult, [("R1", dkt, r), "WQ"], [qk])

                def o_seg(r):
                    c0, c1 = 2 * r, 2 * r + 1
                    par = r % 2
                    fin_k = ("FBF", par * 2)
                    fc1_k = ("FBF", par * 2 + 1)
                    qf, qb, qk = QSETS[par]
                    for ci, tti in enumerate((c0, c1)):
                        pt, ptk = PTS[tti]
                        pairs = [(pt, V[:, tti, :])]
                        rd = [ptk, ("V", tti), qk]
                        if ci == 0:
                            sf = (FBF[:, par * 2], fin_k) if r < 4 else None
                            sbk = (BST[:, 2 * r + 1], ("BST", 2 * r + 1)) if r < 4 else (BST[:, 8], ("BST", 8))
                        else:
                            sf = (FBF[:, par * 2 + 1], fc1_k)
                            sbk = (BST[:, 2 * r], ("BST", 2 * r)) if r < 4 else None
                        if sf is not None:
                            pairs += [(qf[:, dkt, ci * 128:(ci + 1) * 128], sf[0][:, dkt, :]) for dkt in range(2)]
                            rd.append(sf[1])
                        if sbk is not None:
                            pairs += [(qb[:, dkt, ci * 128:(ci + 1) * 128], sbk[0][:, dkt, :]) for dkt in range(2)]
                            rd.append(sbk[1])
                        po, pok = psn()
                        mmg(po[:], pairs, rd, [pok])
                        b0 = (tti % 4) * 16
                        st6 = SMALL[:, b0:b0 + 6]
                        mv = SMALL[:, b0 + 6:b0 + 8]
                        rsd = SMALL[:, b0 + 8:b0 + 9]
                        smv = ("SMv", tti % 4)
                        smr = ("SMr", tti % 4)
                        s.op("dve", (lambda e, st6=st6, po=po: e.bn_stats(out=st6, in_=po[:])), [pok], [smv])
                        s.op("dve", (lambda e, st6=st6, mv=mv: e.bn_aggr(out=mv, in_=st6)), [smv], [smv])
                        act(rsd, mv[:, 1:2], AF.Sqrt, [smv, "EPSC"], [smr], bias=EPSC, scale=1.0)
                        on, onk = tf()
                        stt(on[:], po[:], mv[:, 0:1], SG[:, tti, :], ALU.subtract, ALU.mult, [pok, smv, ("SG", tti)], [onk])
                        s.op("dve", (lambda e, rsd=rsd: e.reciprocal(out=rsd, in_=rsd)), [smr], [smr])
                        act(SG[:, tti, :], on[:], AF.Identity, [onk, smr], [("SG", tti)], scale=rsd)

                def zt(tti):
                    rg = tti % 2
                    psb = psbs[rg]
                    for c in range(4):
                        tr(psb[:, c * 128:(c + 1) * 128], SG[:, tti, c * 128:(c + 1) * 128],
                           [("SG", tti), "IDB"], [("ps", 6 + rg)], signal=(c == 3))
                    for c in range(4):
                        gc = PRM[:, ogn + j * 16 + h * 4 + c:ogn + j * 16 + h * 4 + c + 1]
                        act(V[:, tti, c * 128:(c + 1) * 128], psb[:, c * 128:(c + 1) * 128], AF.Identity,
                            [("ps", 6 + rg), "PRM"], [("V", tti)], scale=gc)

                def fin_copy(r):
                    par = r % 2
                    for dkt in range(2):
                        act(FBF[:, par * 2, dkt, :], FF[:, dkt, :], AF.Identity, [("FST", 0, dkt), "PRM"], [("FBF", par * 2)], scale=isS)

                FB = FST[1]
                gt = 0
                for dkt in range(2):
                    act(BST[:, 6, dkt, :], FB[:, dkt, :], AF.Identity, [("FST", 1, dkt), "PRM"], [("BST", 6)], scale=isS)
                for r in (3, 2, 1, 0):
                    proj(1, gt)
                    proj(1, gt + 1)
                    gt += 2
                    c0, c1 = 2 * r, 2 * r + 1
                    pa = kv_group(c1, 1, None, None)
                    pbb = kv_group(c1, 2, c0, 2)
                    for dkt in range(2):
                        fk = ("FST", 1, dkt)
                        stt(BST[:, 2 * r + 1, dkt, :], FB[:, dkt, :], COLS[:, h, 6:7], pa[dkt][0][:], ALU.mult, ALU.add,
                            [fk, "COLS", pa[dkt][1]], [("BST", 2 * r + 1)])
                    for dkt in range(2):
                        fk = ("FST", 1, dkt)
                        stt(FB[:, dkt, :], FB[:, dkt, :], COLS[:, h, 7:8], pbb[dkt][0][:], ALU.mult, ALU.add,
                            [fk, "COLS", pbb[dkt][1]], [fk])
                    st_out(r, 1, FB)
                    if r == 2:
                        fin_copy(0)
                        fwd_chain(0)
                        fin_copy(1)
                    if r > 0:
                        for dkt in range(2):
                            act(BST[:, 2 * (r - 1), dkt, :], FB[:, dkt, :], AF.Identity, [("FST", 1, dkt), "PRM"],
                                [("BST", 2 * (r - 1))], scale=isS)
                proj(1, 8)
                proj(1, 9)
                pa = kv_group(9, 1, None, None)
                pbb = kv_group(9, 2, 8, 2)
                for dkt in range(2):
                    cpy("act", BST[:, 8, dkt, :], pa[dkt][0][:], [pa[dkt][1]], [("BST", 8)])
                    cpy("dve", FB[:, dkt, :], pbb[dkt][0][:], [pbb[dkt][1]], [("FST", 1, dkt)])
                st_out(4, 1, FB)

                ring["tfn"] = 3
                ring["wn"] = 6
                if h + 1 < H:
                    a1_dma(h + 1)
                WOUT = AR[:, 15360:15360 + 4096]
                for r in range(NSEG):
                    if r + 1 < NSEG:
                        fwd_chain(r + 1)
                    if r + 1 == NSEG - 1:
                        dma("pool", WOUT, rwout_d[j, h], "wout", writes=[("KS", t_) for t_ in range(10)], mdl=4096)
                        if h + 1 < H:
                            wb_dma(h + 1)
                    o_seg(r)
                    if r + 2 <= 3:
                        fin_copy(r + 2)
                    if r >= 1:
                        zt(2 * r - 2)
                        zt(2 * r - 1)
                ring["wn"] = 8
                ring["tfn"] = 4
                ring["wide"] = False
                if h + 1 < H:
                    a1(h + 1)
                zt(8)
                zt(9)
                stage(7)
                stage(8)
                wv = WOUT.rearrange("p (c n) -> p c n", c=4)
                if h == H - 1:
                    mod_drain(99)
                    snap["s"] = s.snapshot()
                order = [(ft, bi) for ft in range(8) for bi in range(3)] if h < H - 1 else [(ft, bi) for bi in range(3) for ft in range(8)]
                for (ft, bi) in order:
                    if True:
                        t0, n = BLKS[bi]
                        cj = 0 if bi < 2 else 1
                        po, pok = psn()
                        nt = n // 128
                        tt0 = t0 // 128
                        mmg(po[:, :n].rearrange("p (t i) -> p t i", t=nt),
                            [(wv[:, c, ft * 128:(ft + 1) * 128], V[:, tt0:tt0 + nt, c * 128:(c + 1) * 128]) for c in range(4)],
                            [("KS", t_) for t_ in range(10)] + [("V", t_) for t_ in range(tt0, tt0 + nt)], [pok])
                        stt(XT[:, ft, t0:t0 + n], po[:, :n], GTc(ft, cj), XT[:, ft, t0:t0 + n], ALU.mult, ALU.add,
                            [pok, ("MOD", cur["l"]), ("XT", ft, bi)], [("XT", ft, bi)])
                        if h == H - 1 and ft == 7:
                            next_rms(bi)

        ADA0 = AR[:, 0:8 * 3072].rearrange("p (k n) -> p k n", k=8)
        av0 = adaw_d[0].rearrange("(kt p) n -> p kt n", p=128)
        for q in range(6):
            dma("pool", ADA0[:, :, q * 512:(q + 1) * 512], av0[:, :, q * 512:(q + 1) * 512], f"ada0_{q}", writes=[("ADA0", q)])
        for ot in range(24):
            pm, pmk = psn()
            mmg(pm[:, 0:2], [(ADA0[:, kt, ot * 128:(ot + 1) * 128], SCT[:, kt, :]) for kt in range(8)],
                reads=[("ADA0", ot // 4), "SCT"], writes=[pmk])
            o_, _w = POFF["adab"]
            adab = PRM[:, o_ + ot:o_ + ot + 1].broadcast_to([128, 2])
            tt("dve", MODS[0][:, ot, :], pm[:, 0:2], adab, ALU.add, [pmk, "PRM"], [("MOD", 0)])
        mod_finish(0)
        s.barrier()
        for l in range(n_layers):
            cur["l"] = l
            mod_start(l + 1)
            try:
                ring["nar"] = [0, 1, 2, 3, 4, 6, 7] if l % 2 == 0 else [0, 1, 2, 3, 4]
                if l % 2 == 0:
                    conv_layer(l, l // 2)
                else:
                    ret_layer(l, l // 2)
            except _Stop:
                pass
            mod_drain(99)
            if snap["s"] is not None:
                s.barrier_from(snap["s"])
                snap["s"] = None
            else:
                s.barrier()

        with nc.Block() as block:
            s.emit(block)
    return nc


def _core_slots(c):
    if c < 2:
        return [("s", c, r) for r in range(4)] + [("p", c)]
    return [("p", 2 + (c - 2) * 5 + r) for r in range(5)]


_NC_CACHE = {}


def _run(inputs, n_layers=4):
    inp = {k: np.asarray(v) for k, v in inputs.items()}
    f32 = np.float32
    xp, xs = inp["x_prompt"].astype(f32), inp["x_sample"].astype(f32)
    cwin = np.ascontiguousarray(inp["conv_w_in"].reshape(2, 8, 128, 3, 16, 128).transpose(0, 4, 2, 3, 1, 5)).reshape(2, 16, 128, 3072)
    cwout = np.ascontiguousarray(inp["conv_w_out"].reshape(2, 16, 128, 8, 128).transpose(0, 3, 2, 1, 4)).reshape(2, 8, 128, 2048)
    rw = inp["ret_w_in"]
    rwqk = np.ascontiguousarray(rw[:, :, 0:2048].reshape(2, 8, 128, 16, 128).transpose(0, 3, 2, 1, 4)).reshape(2, 16, 128, 1024)
    rwvg = np.ascontiguousarray(rw[:, :, 2048:].reshape(2, 8, 128, 8, 512).transpose(0, 3, 2, 1, 4)).reshape(2, 8, 128, 4096)
    rwout = np.ascontiguousarray(inp["ret_w_out"].reshape(2, 4, 4, 128, 1024).transpose(0, 1, 3, 2, 4)).reshape(2, 4, 128, 4096)
    adaw = np.ascontiguousarray(inp["ada_w"].astype(f32))
    in_maps = []
    for c in range(8):
        slots = _core_slots(c)
        segs = []
        for sl in slots:
            if sl[0] == "s":
                segs.append(xs[sl[1], sl[2] * SEG:(sl[2] + 1) * SEG])
            else:
                segs.append(xp[sl[1]])
        x = np.concatenate(segs, axis=0)
        is_s = c < 2
        condA = inp["c"][c] if is_s else inp["c_ctx"]
        condB = inp["c_ctx"]
        cond = np.stack([condA, condB], axis=-1).reshape(8, 128, 2).transpose(1, 0, 2).reshape(128, 16)
        ext = inp["state_ret"][c if is_s else 0]
        in_maps.append({
            "xT": np.ascontiguousarray(x.T.astype(f32)),
            "cond": np.ascontiguousarray(cond.astype(f32)),
            "prm": _pack_params(inp, 1.0 if is_s else 0.0),
            "rope": _rope_tables(is_s),
            "ext": np.ascontiguousarray(ext.astype(f32)),
            "adaw": adaw, "cwin": cwin, "cwout": cwout, "rwqk": rwqk, "rwvg": rwvg, "rwout": rwout,
        })
    ck = (n_layers, _STOP)
    if ck not in _NC_CACHE:
        _NC_CACHE[ck] = build_program(n_layers)
    nc = _NC_CACHE[ck]
    res = run_bass_kernel_spmd(nc, in_maps, core_ids=list(range(8)))
    y_prompt = np.zeros((32, 256, D), f32)
    y_sample = np.zeros((2, 1024, D), f32)
    new_state = np.zeros((32, 2, 2, 4, 256, 512), f32)
    for c in range(8):
        y = res.results[c]["yT"].T
        stc = res.results[c]["st"]
        for si, sl in enumerate(_core_slots(c)):
            blk = y[si * SEG:(si + 1) * SEG]
            if sl[0] == "s":
                y_sample[sl[1], sl[2] * SEG:(sl[2] + 1) * SEG] = blk
            else:
                y_prompt[sl[1]] = blk
                new_state[sl[1]] = stc[si]
    return y_prompt, y_sample, new_state


def kernel(**inputs):
    return _run(inputs, 4)
```

```python
import numpy as np
from contextlib import ExitStack
import concourse.bass as bass
import concourse.mybir as mybir
from concourse.bass_utils import run_bass_kernel_spmd

F32 = mybir.dt.float32
BF16 = mybir.dt.bfloat16
AF = mybir.ActivationFunctionType
ALU = mybir.AluOpType

D = 1024
DI = 2048
T = 1280
SEG = 256
NSEG = 5
PAD = 15
KW = 31
H = 4
EPS = 1e-6
BLKS = [(0, 512), (512, 512), (1024, 256)]
SEGP = SEG + 2 * PAD
NDVE = 6


class Sched:
    ENG = ("pe", "act", "dve", "pool", "sp")

    def __init__(self, nc):
        self.nc = nc
        self.ops = {e: [] for e in self.ENG}
        self.sem = {e: nc.alloc_semaphore("s_" + e) for e in self.ENG}
        self.cnt = {e: 0 for e in self.ENG}
        self.dsem = {}
        self.dcnt = {}
        self.seen = {e: {} for e in self.ENG}
        self.last_w = {}
        self.reads = {}
        self.out_events = []
        self.nwaits = 0

    def _waits_for(self, eng, reads, writes):
        need = {}

        def add(ev):
            k, v = ev
            if need.get(k, 0) < v:
                need[k] = v

        for r in reads:
            if r in self.last_w:
                add(self.last_w[r])
        for w in writes:
            if w in self.last_w:
                add(self.last_w[w])
            for ev in self.reads.get(w, ()):
                add(ev)
        res = []
        seen = self.seen[eng]
        for k, v in need.items():
            if seen.get(k, 0) >= v:
                continue
            seen[k] = v
            res.append((k, v))
        self.nwaits += len(res)
        return res

    def _semobj(self, k):
        return self.sem[k] if k in self.sem else self.dsem[k]

    def _record(self, e, reads, writes):
        for r in reads:
            self.reads.setdefault(r, []).append(e)
        for w in writes:
            self.last_w[w] = e
            self.reads[w] = []

    def op(self, eng, fn, reads=(), writes=(), signal=True):
        pk = [r for r in reads if isinstance(r, tuple) and r[0] in ("ps", "psb")]
        if pk:
            writes = list(writes) + [r for r in pk if r not in writes]
        waits = self._waits_for(eng, reads, writes)
        if signal:
            self.cnt[eng] += 1
            self.ops[eng].append((waits, fn, (eng, 1)))
            self._record((eng, self.cnt[eng]), reads, writes)
        else:
            self.ops[eng].append((waits, fn, None))

    def dma(self, q, fn, semkey, reads=(), writes=(), is_output=False):
        if semkey not in self.dsem:
            self.dsem[semkey] = self.nc.alloc_semaphore("d_" + semkey)
            self.dcnt[semkey] = 0
        waits = self._waits_for(q, reads, writes)
        self.dcnt[semkey] += 16
        self.ops[q].append((waits, fn, (semkey, 16)))
        e = (semkey, self.dcnt[semkey])
        self._record(e, reads, writes)
        if is_output:
            self.out_events.append(e)

    def barrier(self):
        for e in ("pe", "act", "dve", "pool"):
            waits = []
            for o in ("pe", "act", "dve", "pool"):
                if o == e or self.cnt[o] == 0:
                    continue
                if self.seen[e].get(o, 0) < self.cnt[o]:
                    self.seen[e][o] = self.cnt[o]
                    waits.append((o, self.cnt[o]))
            if waits:
                self.ops[e].append((waits, None, None))

    def snapshot(self):
        return dict(self.cnt)

    def barrier_from(self, snap):
        comp = ("pe", "act", "dve", "pool")
        for e in comp:
            waits = []
            for o in comp:
                if o == e:
                    continue
                target = self.cnt[o] if o == "pe" else snap[o]
                if target > 0 and self.seen[e].get(o, 0) < target:
                    self.seen[e][o] = target
                    waits.append((o, target))
            if waits:
                self.ops[e].append((waits, None, None))

    def emit(self, block):
        S = self
        need = {}
        for (k, v) in self.out_events:
            need[k] = max(need.get(k, 0), v)
        final_waits = list(need.items())

        def run(engname, eng):
            for (waits, fn, inc) in S.ops[engname]:
                for (k, v) in waits:
                    eng.wait_ge(S._semobj(k), v)
                if fn is None:
                    continue
                ins = fn(eng)
                if inc is not None:
                    ins.then_inc(S._semobj(inc[0]), inc[1])
            if engname == "sp":
                for (k, v) in final_waits:
                    eng.wait_ge(S._semobj(k), v)

        @block.tensor
        def _(e):
            run("pe", e)

        @block.scalar
        def _(e):
            run("act", e)

        @block.vector
        def _(e):
            run("dve", e)

        @block.gpsimd
        def _(e):
            run("pool", e)

        @block.sync
        def _(e):
            run("sp", e)


def _param_layout():
    off = {}
    n = 0
    for name, w in [("adab", 96), ("ng", 32), ("fng", 8), ("dw", 2 * 16 * 31), ("dwb", 32), ("lng", 32),
                    ("lnb", 32), ("gng", 32), ("ld", 16), ("isS", 1), ("c127mp", 1), ("c255mp", 1), ("cp", 1),
                    ("c128pp", 1), ("dpos", 128), ("dneg", 128), ("mge", 128), ("mle", 128), ("rowi1", 128),
                    ("row128mi", 128), ("ident", 128)]:
        off[name] = (n, w)
        n += w
    return off, n


POFF, NPRM = _param_layout()


def _pack_params(inp, isS):
    P = np.zeros((128, NPRM), np.float32)

    def put(name, arr):
        o, w = POFF[name]
        P[:, o:o + w] = np.asarray(arr, np.float32).reshape(128, w)

    p = np.arange(128, dtype=np.float32)
    put("adab", inp["ada_b"].reshape(4, 24, 128).transpose(2, 0, 1))
    put("ng", inp["norm_g"].reshape(4, 8, 128).transpose(2, 0, 1))
    put("fng", inp["final_norm_g"].reshape(8, 128).transpose(1, 0))
    put("dw", inp["conv_dw"].reshape(2, 31, 16, 128).transpose(3, 0, 2, 1))
    put("dwb", inp["conv_dw_b"].reshape(2, 16, 128).transpose(2, 0, 1))
    put("lng", inp["conv_ln_g"].reshape(2, 16, 128).transpose(2, 0, 1))
    put("lnb", inp["conv_ln_b"].reshape(2, 16, 128).transpose(2, 0, 1))
    put("gng", inp["ret_gn_g"].reshape(2, 16, 128).transpose(2, 0, 1))
    put("ld", np.broadcast_to(inp["ret_log_decay"].reshape(1, 16), (128, 16)))
    put("isS", np.full((128, 1), isS))
    put("c127mp", 127 - p)
    put("c255mp", 255 - p)
    put("cp", p)
    put("c128pp", 128 + p)
    jj = p[:, None]
    ii = p[None, :]
    put("dpos", np.maximum(ii - jj, 0))
    put("dneg", np.maximum(jj - ii, 0))
    put("mge", (ii >= jj))
    put("mle", (jj >= ii))
    put("rowi1", np.broadcast_to(ii + 1, (128, 128)))
    put("row128mi", np.broadcast_to(128 - ii, (128, 128)))
    put("ident", np.eye(128))
    return P


def _rope_tables(is_sample):
    tab = np.zeros((128, 2, 1024), np.float32)
    if not is_sample:
        tab[:, 0, :] = 1.0
        return tab
    L = 1024
    GW = 64
    rows = L // GW
    r = np.repeat(np.arange(rows, dtype=np.float32), GW)
    col = np.tile(np.arange(GW, dtype=np.float32), rows)
    nf = 64
    inv = (np.float32(10000.0) ** (-np.arange(nf, dtype=np.float32) / np.float32(nf))).astype(np.float32)
    ang = np.concatenate([r[:, None] * inv, col[:, None] * inv], axis=-1).astype(np.float32)
    tab[:, 0, :] = np.cos(ang).T
    tab[:, 1, :] = np.sin(ang).T
    return tab


class _Stop(Exception):
    pass


_STOP = 99


def build_program(n_layers=4):
    nc = bass.Bass("TRN2", target_bir_lowering=False)

    def stage(k):
        if _STOP <= k:
            raise _Stop()

    def din(name, shape):
        return nc.dram_tensor(name, shape, F32, kind="ExternalInput").ap()

    xT_d = din("xT", [D, T])
    cond_d = din("cond", [128, 16])
    prm_d = din("prm", [128, NPRM])
    rope_d = din("rope", [128, 2, 1024])
    ext_d = din("ext", [2, 2, 4, 256, 512])
    adaw_d = din("adaw", [4, D, 3 * D])
    cwin_d = din("cwin", [2, 16, 128, 3072])
    cwout_d = din("cwout", [2, 8, 128, 2048])
    rwqk_d = din("rwqk", [2, 16, 128, 1024])
    rwvg_d = din("rwvg", [2, 8, 128, 4096])
    rwout_d = din("rwout", [2, 4, 128, 4096])
    yT_d = nc.dram_tensor("yT", [D, T], F32, kind="ExternalOutput").ap()
    st_d = nc.dram_tensor("st", [NSEG, 2, 2, 4, 256, 512], F32, kind="ExternalOutput").ap()

    with ExitStack() as es:
        def sb(name, shape, dt):
            return es.enter_context(nc.sbuf_tensor(name, shape, dt))

        XT = sb("XT", [128, 8, T], F32)
        HT = sb("HT", [128, 8, T], BF16)
        PRM = sb("PRM", [128, NPRM], F32)
        CONDT = sb("CONDT", [128, 16], F32)
        SCT = sb("SCT", [128, 8, 2], BF16)
        IDB = sb("IDB", [128, 128], BF16)
        ONESD = sb("ONESD", [128, 128], BF16)
        ONESI = sb("ONESI", [128, 128], BF16)
        MODS = [sb(f"MOD{i}", [128, 24, 2], F32) for i in range(4)]
        GMS = [sb(f"GM{i}", [128, 8, 2], F32) for i in range(4)]
        cur = {"l": 0}
        pending = []
        WA = [sb(f"WA{i}", [128, 2, 8, 128], BF16) for i in range(2)]
        WB = [sb(f"WB{i}", [128, 4096], BF16) for i in range(2)]
        ADA = [sb(f"ADA{i}", [128, 8, 128], BF16) for i in range(2)]
        RSB = sb("RSB", [128, 512], F32)
        TF = [sb(f"TF{i}", [128, 512], F32) for i in range(4)]
        TB = [sb(f"TB{i}", [128, 512], BF16) for i in range(4)]
        FST = [sb(f"FST{i}", [128, 2, 512], F32) for i in range(2)]
        SMALL = sb("SMALL", [128, 64], F32)
        LG = sb("LG", [128, 8], F32)
        COLS = sb("COLS", [128, 4, 8], F32)
        ARF = sb("ARF", [128, 3584], F32)
        ARN = 37376
        AR = sb("AR", [128, ARN], BF16)
        ps = [es.enter_context(nc.psum_tensor(f"ps{i}", [128, 512], F32)) for i in range(8)]
        psbs = [ps[6][:].bitcast(BF16), ps[7][:].bitcast(BF16)]
        PSTAT = 5

        s = Sched(nc)

        def P_(name, a=0, w=None):
            o, ww = POFF[name]
            if w is None:
                w = ww - a
            return PRM[:, o + a:o + a + w]

        def act(out, in_, func, reads, writes, scale=None, bias=None):
            kw = {}
            if scale is not None:
                kw["scale"] = scale
            if bias is not None:
                kw["bias"] = bias
            s.op("act", lambda e: e.activation(out=out, in_=in_, func=func, **kw), reads, writes)

        def tt(eng, out, in0, in1, op, reads, writes):
            s.op(eng, lambda e: e.tensor_tensor(out=out, in0=in0, in1=in1, op=op), reads, writes)

        def ts(eng, out, in0, s1, op0, reads, writes, s2=None, op1=None):
            if op1 is None:
                s.op(eng, lambda e: e.tensor_scalar(out=out, in0=in0, scalar1=s1, scalar2=None, op0=op0), reads, writes)
            else:
                s.op(eng, lambda e: e.tensor_scalar(out=out, in0=in0, scalar1=s1, scalar2=s2, op0=op0, op1=op1), reads, writes)

        def stt(out, in0, scalar, in1, op0, op1, reads, writes):
            s.op("dve", lambda e: e.scalar_tensor_tensor(out=out, in0=in0, scalar=scalar, in1=in1, op0=op0, op1=op1), reads, writes)

        def mmg(out, pairs, reads, writes, transpose=False):
            n = len(pairs)
            for i, (l, r) in enumerate(pairs):
                s.op("pe", (lambda e, l=l, r=r, i=i: e.matmul(out, lhsT=l, rhs=r, start=(i == 0), stop=(i == n - 1))),
                     reads, writes, signal=(i == n - 1))

        def tr(out, in_, reads, writes, signal=True):
            s.op("pe", lambda e: e.transpose(out=out, in_=in_, identity=IDB[:]), reads, writes, signal=signal)

        def cpy(eng, out, in_, reads, writes):
            if eng == "act":
                act(out, in_, AF.Copy, reads, writes)
            else:
                s.op(eng, lambda e: e.tensor_copy(out=out, in_=in_), reads, writes)

        def dma(q, out, in_, semkey, reads=(), writes=(), is_output=False, mdl=None):
            if mdl is None:
                s.dma(q, lambda e: e.dma_start(out=out, in_=in_), semkey, reads, writes, is_output)
            else:
                s.dma(q, lambda e: e.dma_start(out=out, in_=in_, max_dma_last_dim=mdl), semkey, reads, writes, is_output)

        ring = {"tf": 0, "tb": 0, "ps": 0, "psw": 0, "wide": False, "tfn": 4, "wn": 8, "nar": [0, 1, 2, 3, 4]}

        def tf():
            i = ring["tf"]
            i = i % ring["tfn"]
            ring["tf"] = (i + 1) % ring["tfn"]
            return TF[i], ("tf", i)

        def tb():
            i = ring["tb"]
            ring["tb"] = (i + 1) % 4
            return TB[i], ("tb", i)

        def psn():
            if ring["wide"]:
                i = ring["psw"]
                i = i % ring["wn"]
                ring["psw"] = (i + 1) % ring["wn"]
                return ps[i], ("ps", i)
            nar = ring["nar"]
            i = ring["ps"] % len(nar)
            ring["ps"] = (i + 1) % len(nar)
            return ps[nar[i]], ("ps", nar[i])

        dma("sp", PRM[:], prm_d[:], "prm", writes=["PRM"])
        dma("sp", CONDT[:], cond_d[:], "cond", writes=["CONDT"])
        xT_v = xT_d.rearrange("(kt p) t -> p kt t", p=128)
        for kt in range(8):
            dma("sp", XT[:, kt, :], xT_v[:, kt, :], f"x{kt}", writes=[("XT", kt, b) for b in range(3)])
        EPSC = SMALL[:, 63:64]
        s.op("pool", lambda e: e.memset(EPSC, EPS), writes=["EPSC"])
        s.op("pool", lambda e: e.memset(ONESD[:], 1.0 / D), writes=["ONESD"])
        s.op("pool", lambda e: e.memset(ONESI[:], 1.0 / DI), writes=["ONESI"])
        cpy("dve", IDB[:], P_("ident"), ["PRM"], ["IDB"])
        act(SCT[:].rearrange("p k j -> p (k j)"), CONDT[:], AF.Silu, ["CONDT"], ["SCT"])

        def mod_group(l, ot):
            av = adaw_d[l].rearrange("(kt p) n -> p kt n", p=128)
            sl = ot % 2
            dma("pool", ADA[sl][:], av[:, :, ot * 128:(ot + 1) * 128], f"ada{sl}", writes=[("ADA", sl)])
            pm, pmk = psn()
            mmg(pm[:, 0:2], [(ADA[sl][:, kt, :], SCT[:, kt, :]) for kt in range(8)],
                reads=[("ADA", sl), "SCT"], writes=[pmk])
            o_, _w = POFF["adab"]
            adab = PRM[:, o_ + l * 24 + ot:o_ + l * 24 + ot + 1].broadcast_to([128, 2])
            tt("dve", MODS[l][:, ot, :], pm[:, 0:2], adab, ALU.add, [pmk, "PRM"], [("MOD", l)])

        def mod_finish(l):
            o_, _w = POFF["ng"]
            ngb = PRM[:, o_ + l * 8:o_ + (l + 1) * 8].unsqueeze(2).broadcast_to([128, 8, 2])
            ts("dve", GMS[l][:], MODS[l][:, 8:16, :], 1.0, ALU.add, [("MOD", l)], [("GM", l)])
            tt("dve", GMS[l][:], GMS[l][:], ngb, ALU.mult, [("GM", l), "PRM"], [("GM", l)])

        def mod_start(l):
            if l < n_layers:
                pending.extend((l, ot) for ot in range(24))

        def mod_drain(k):
            while k > 0 and pending:
                l, ot = pending.pop(0)
                mod_group(l, ot)
                if ot == 23:
                    mod_finish(l)
                k -= 1

        def SHc(kt, cj):
            return MODS[cur["l"]][:, kt, cj:cj + 1]

        def GTc(kt, cj):
            return MODS[cur["l"]][:, 16 + kt, cj:cj + 1]

        def rms_block(bi, gfun, outfun):
            t0, n = BLKS[bi]
            for kt in range(8):
                sq, sqk = tb()
                act(sq[:, :n], XT[:, kt, t0:t0 + n], AF.Square, [("XT", kt, bi)], [sqk])
                s.op("pe", (lambda e, kt=kt, sq=sq: e.matmul(ps[PSTAT][:, :n], lhsT=ONESD[:], rhs=sq[:, :n], start=(kt == 0), stop=(kt == 7))),
                     [sqk, "ONESD"], [("ps", PSTAT)], signal=True)
            rs, rsk = RSB, "RSB"
            act(rs[:, :n], ps[PSTAT][:, :n], AF.Sqrt, [("ps", PSTAT), "EPSC"], [rsk], bias=EPSC, scale=1.0)
            s.op("dve", lambda e: e.reciprocal(out=rs[:, :n], in_=rs[:, :n]), [rsk], [rsk])
            for kt in range(8):
                tmp, tk = tf()
                stt(tmp[:, :n], XT[:, kt, t0:t0 + n], gfun(kt), rs[:, :n], ALU.mult, ALU.mult,
                    [("XT", kt, bi), rsk, ("GM", cur["l"]), "PRM"], [tk])
                outfun(kt, tmp, tk)

        def rms_to_HT_blk(bi):
            t0, n = BLKS[bi]
            cj = 0 if bi < 2 else 1

            def outf(kt, tmp, tk):
                act(HT[:, kt, t0:t0 + n], tmp[:, :n], AF.Identity, [tk, ("MOD", cur["l"])], [("HT", kt, bi)],
                    bias=SHc(kt, cj), scale=1.0)

            rms_block(bi, lambda kt: GMS[cur["l"]][:, kt, cj:cj + 1], outf)

        def rms_to_HT():
            arfk = [("RS", b_) for b_ in range(3)] + [("NMR", b_) for b_ in range(3)]
            RSv = [ARF[:, b_ * 512:(b_ + 1) * 512] for b_ in range(3)]
            banks = [(ps[PSTAT], ("ps", PSTAT)), psn(), psn()]
            for bi, (t0, n) in enumerate(BLKS):
                pb_, pbk_ = banks[bi]
                for kt in range(8):
                    sq, sqk = tb()
                    if kt % 2 == 1:
                        tt("dve", sq[:, :n], XT[:, kt, t0:t0 + n], XT[:, kt, t0:t0 + n], ALU.mult, [("XT", kt, bi)], [sqk])
                    else:
                        act(sq[:, :n], XT[:, kt, t0:t0 + n], AF.Square, [("XT", kt, bi)], [sqk])
                    s.op("pe", (lambda e, kt=kt, sq=sq, n=n, pb_=pb_: e.matmul(pb_[:, :n], lhsT=ONESD[:], rhs=sq[:, :n], start=(kt == 0), stop=(kt == 7))),
                         [sqk, "ONESD"], [pbk_], signal=True)
            for bi, (t0, n) in enumerate(BLKS):
                pb_, pbk_ = banks[bi]
                rs = RSv[bi]
                act(rs[:, :n], pb_[:, :n], AF.Sqrt, [pbk_, "EPSC"], arfk, bias=EPSC, scale=1.0)
                s.op("dve", (lambda e, rs=rs, n=n: e.reciprocal(out=rs[:, :n], in_=rs[:, :n])), arfk, arfk)
            for bi, (t0, n) in enumerate(BLKS):
                cj = 0 if bi < 2 else 1
                rs = RSv[bi]
                for kt in range(8):
                    tmp, tk = tf()
                    stt(tmp[:, :n], XT[:, kt, t0:t0 + n], GMS[cur["l"]][:, kt, cj:cj + 1], rs[:, :n], ALU.mult, ALU.mult,
                        [("XT", kt, bi), ("GM", cur["l"]), "PRM"] + arfk, [tk])
                    act(HT[:, kt, t0:t0 + n], tmp[:, :n], AF.Identity, [tk, ("MOD", cur["l"])], [("HT", kt, bi)],
                        bias=SHc(kt, cj), scale=1.0)

        def final_norm_blk(bi):
            yv = yT_d.rearrange("(kt p) t -> p kt t", p=128)
            o_, _w = POFF["fng"]
            t0, n = BLKS[bi]

            def outf(kt, tmp, tk):
                dma("sp", yv[:, kt, t0:t0 + n], tmp[:, :n], f"y{tk[1]}", reads=[tk], is_output=True)

            rms_block(bi, lambda kt: PRM[:, o_ + kt:o_ + kt + 1], outf)

        def next_rms(bi):
            l = cur["l"]
            if l + 1 < n_layers:
                assert not [p_ for p_ in pending if p_[0] == l + 1]
                cur["l"] = l + 1
                rms_to_HT_blk(bi)
                cur["l"] = l
            else:
                final_norm_blk(bi)

        prep_done = set()
        snap = {"s": None}

        def qk_dma(j, h):
            for which in range(2):
                for dkt in range(2):
                    ot = which * 8 + h * 2 + dkt
                    if which == 0:
                        dma("pool", WA[dkt][:, 0].rearrange("p k m -> p (k m)"), rwqk_d[j, ot], f"wA{dkt}",
                            writes=[("wA", dkt)], mdl=4096)
                    else:
                        dma("pool", ADA[dkt][:].rearrange("p k m -> p (k m)"), rwqk_d[j, ot], f"ada{dkt}",
                            writes=[("ADA", dkt)], mdl=4096)

        def ret_prep(j):
            prep_done.add(j)
            COS = ARF[:, 0:1024]
            SIN = ARF[:, 1024:2048]
            DT = ARF[:, 2048:2560].rearrange("p (h i) -> p h i", h=4)
            ARF16 = ARF[:].bitcast(BF16)
            WQF = ARF16[:, 5120:5632].rearrange("p (h i) -> p h i", h=4)
            WQB = ARF16[:, 5632:6144].rearrange("p (h i) -> p h i", h=4)
            isS = P_("isS")
            qk_dma(j, 0)
            dma("sp", ARF[:, 0:2048], rope_d.rearrange("p a t -> p (a t)"), "rope",
                writes=["rope"] + [("RS", b) for b in range(3)] + [("NMR", b) for b in range(3)])
            old, _w = POFF["ld"]
            act(LG[:], PRM[:, old + j * 8:old + (j + 1) * 8], AF.Exp, ["PRM"], ["LG"])
            ts("dve", LG[:], LG[:], -1.0, ALU.mult, ["LG"], ["LG"])
            for h in range(H):
                lgf = LG[:, h:h + 1]
                lgb = LG[:, 4 + h:5 + h]
                e1, e1k = tf()
                e2, e2k = tf()
                act(e1[:, 0:128], P_("dpos"), AF.Exp, ["PRM", "LG"], [e1k], scale=lgf)
                tt("dve", e1[:, 0:128], e1[:, 0:128], P_("mge"), ALU.mult, [e1k, "PRM"], [e1k])
                act(e2[:, 0:128], P_("dneg"), AF.Exp, ["PRM", "LG"], [e2k], scale=lgb)
                tt("dve", e2[:, 0:128], e2[:, 0:128], P_("mle"), ALU.mult, [e2k, "PRM"], [e2k])
                tt("dve", DT[:, h, :], e1[:, 0:128], e2[:, 0:128], ALU.add, [e1k, e2k], ["DT"] + [("NMR", b_) for b_ in range(3)])
                act(WQF[:, h, :], P_("rowi1"), AF.Exp, ["PRM", "LG"], ["WQ"], scale=lgf)
                act(WQB[:, h, :], P_("row128mi"), AF.Exp, ["PRM", "LG"], ["WQ"], scale=lgb)
                act(COLS[:, h, 0:1], P_("c127mp"), AF.Exp, ["PRM", "LG"], ["COLS"], scale=lgf)
                act(COLS[:, h, 1:2], P_("c255mp"), AF.Exp, ["PRM", "LG"], ["COLS"], scale=lgf)
                act(COLS[:, h, 2:3], P_("cp"), AF.Exp, ["PRM", "LG"], ["COLS"], scale=lgb)
                act(COLS[:, h, 3:4], P_("c128pp"), AF.Exp, ["PRM", "LG"], ["COLS"], scale=lgb)
                act(COLS[:, h, 4:5], lgf, AF.Exp, ["LG"], ["COLS"], scale=128.0)
                act(COLS[:, h, 5:6], lgf, AF.Exp, ["LG"], ["COLS"], scale=256.0)
                act(COLS[:, h, 6:7], lgb, AF.Exp, ["LG"], ["COLS"], scale=128.0)
                act(COLS[:, h, 7:8], lgb, AF.Exp, ["LG"], ["COLS"], scale=256.0)
                ts("dve", COLS[:, h, 4:8], COLS[:, h, 4:8], isS, ALU.mult, ["COLS", "PRM"], ["COLS"])

        def conv_layer(l, j):
            C = AR[:, 0:16 * T].rearrange("p (c t) -> p c t", c=16)
            DG = [AR[:, 20480 + i * 3968:20480 + (i + 1) * 3968].rearrange("p (k m) -> p k m", k=KW) for i in range(2)]
            UP = [AR[:, 28416 + i * 1430:28416 + (i + 1) * 1430].rearrange("p (s t) -> p s t", s=NSEG) for i in range(2)]
            RS = ARF[:, 0:T]
            NMR = ARF[:, T:2 * T]
            isS = P_("isS")
            for i in range(2):
                s.op("dve", (lambda e, i=i: e.memset(UP[i], 0.0)), writes=[("up", i)])
            if l == 0:
                rms_to_HT()
            odw, _w = POFF["dw"]
            odwb, _w = POFF["dwb"]
            dma("pool", WA[0][:].rearrange("p w k m -> p (w k m)"), cwin_d[j, 0, :, 0:2048], "wA0",
                writes=[("wA", 0)], mdl=4096)

            def ab(ct):
                sl = ct % 2
                if ct + 1 < 16:
                    dma("pool", WA[1 - sl][:].rearrange("p w k m -> p (w k m)"), cwin_d[j, ct + 1, :, 0:2048], f"wA{1 - sl}",
                        writes=[("wA", 1 - sl)], mdl=4096)
                dwc = PRM[:, odw + (j * 16 + ct) * KW + NDVE:odw + (j * 16 + ct + 1) * KW]
                tt("dve", DG[sl][:, NDVE:KW, :], IDB[:].unsqueeze(1).broadcast_to([128, KW - NDVE, 128]),
                   dwc.unsqueeze(2).broadcast_to([128, KW - NDVE, 128]), ALU.mult, ["IDB", "PRM"], [("dg", sl)])

            def ab_blk(ct, bi):
                sl = ct % 2
                if True:
                    t0, n = BLKS[bi]
                    ns = n // SEG
                    pa, pak = psn()
                    pb, pbk = psn()
                    hk = [("HT", kt, bi) for kt in range(8)]
                    mmg(pa[:, :n], [(WA[sl][:, 0, kt, :], HT[:, kt, t0:t0 + n]) for kt in range(8)], [("wA", sl)] + hk, [pak])
                    mmg(pb[:, :n], [(WA[sl][:, 1, kt, :], HT[:, kt, t0:t0 + n]) for kt in range(8)], [("wA", sl)] + hk, [pbk])
                    sg, sgk = tf()
                    act(sg[:, :n], pb[:, :n], AF.Sigmoid, [pbk], [sgk])
                    tt("dve", UP[sl][:, 2 * bi:2 * bi + ns, PAD:PAD + SEG],
                       pa[:, :n].rearrange("p (s t) -> p s t", s=ns), sg[:, :n].rearrange("p (s t) -> p s t", s=ns),
                       ALU.mult, [pak, sgk], [("up", sl)])

            def pads(ct):
                sl = ct % 2
                for r in range(1, 4):
                    ts("dve", UP[sl][:, r, 0:PAD], UP[sl][:, r - 1, SEG:SEG + PAD], isS, ALU.mult, [("up", sl), "PRM"], [("up", sl)])
                for r in range(0, 3):
                    ts("dve", UP[sl][:, r, PAD + SEG:SEGP], UP[sl][:, r + 1, PAD:2 * PAD], isS, ALU.mult, [("up", sl), "PRM"], [("up", sl)])

            def dwconv_blk(ct, bi):
                sl = ct % 2
                for _ in range(1):
                    t0, n = BLKS[bi]
                    ns = n // SEG
                    pc, pck = psn()
                    mmg(pc[:, :n].rearrange("p (s t) -> p s t", s=ns),
                        [(DG[sl][:, k, :], UP[sl][:, 2 * bi:2 * bi + ns, k:k + SEG]) for k in range(NDVE, KW)],
                        [("dg", sl), ("up", sl)], [pck])
                    bcol = PRM[:, odwb + j * 16 + ct:odwb + j * 16 + ct + 1]
                    if NDVE == 0:
                        act(C[:, ct, t0:t0 + n], pc[:, :n], AF.Identity, [pck, "PRM"], [("C", ct, bi)], bias=bcol, scale=1.0)
                        continue
                    acc, acck = tf()
                    accv = acc[:, :n].rearrange("p (s t) -> p s t", s=ns)
                    for k in range(NDVE):
                        dcol = PRM[:, odw + (j * 16 + ct) * KW + k:odw + (j * 16 + ct) * KW + k + 1]
                        uv = UP[sl][:, 2 * bi:2 * bi + ns, k:k + SEG]
                        if k == 0:
                            ts("dve", accv, uv, dcol, ALU.mult, [("up", sl), "PRM"], [acck])
                        else:
                            stt(accv, uv, dcol, accv, ALU.mult, ALU.add, [("up", sl), "PRM", acck], [acck])
                    stt(C[:, ct, t0:t0 + n], pc[:, :n], bcol, acc[:, :n], ALU.add, ALU.add, [pck, "PRM", acck], [("C", ct, bi)])

            ab(0)
            for bi in range(3):
                ab_blk(0, bi)
            pads(0)
            for ct in range(16):
                if ct + 1 < 16:
                    ab(ct + 1)
                for bi in range(3):
                    if ct + 1 < 16:
                        ab_blk(ct + 1, bi)
                        if bi == 2:
                            pads(ct + 1)
                    dwconv_blk(ct, bi)
                    if bi < 2:
                        mod_drain(1)
            sbank = [psn() for _ in range(3)]
            qbank = [psn() for _ in range(3)]
            for bi, (t0, n) in enumerate(BLKS):
                p1, p1k = sbank[bi]
                p2, p2k = qbank[bi]
                for ct in range(16):
                    s.op("pe", (lambda e, ct=ct, t0=t0, n=n, p1=p1: e.matmul(p1[:, :n], lhsT=ONESI[:], rhs=C[:, ct, t0:t0 + n], start=(ct == 0), stop=(ct == 15))),
                         [("C", ct, bi), "ONESI"], [p1k], signal=(ct == 15))
                for ct in range(16):
                    sq, sqk = tb()
                    if ct % 3 == 2:
                        tt("dve", sq[:, :n], C[:, ct, t0:t0 + n], C[:, ct, t0:t0 + n], ALU.mult, [("C", ct, bi)], [sqk])
                    else:
                        act(sq[:, :n], C[:, ct, t0:t0 + n], AF.Square, [("C", ct, bi)], [sqk])
                    s.op("pe", (lambda e, ct=ct, sq=sq, n=n, p2=p2: e.matmul(p2[:, :n], lhsT=ONESI[:], rhs=sq[:, :n], start=(ct == 0), stop=(ct == 15))),
                         [sqk, "ONESI"], [p2k], signal=True)
            for bi, (t0, n) in enumerate(BLKS):
                p1, p1k = sbank[bi]
                p2, p2k = qbank[bi]
                mu, muk = tf()
                cpy("dve", mu[:, :n], p1[:, :n], [p1k], [muk])
                m2, m2k = tf()
                tt("dve", m2[:, :n], mu[:, :n], mu[:, :n], ALU.mult, [muk], [m2k])
                tt("dve", m2[:, :n], p2[:, :n], m2[:, :n], ALU.subtract, [p2k, m2k], [m2k])
                ts("dve", m2[:, :n], m2[:, :n], 0.0, ALU.max, [m2k], [m2k], s2=EPS, op1=ALU.add)
                act(m2[:, :n], m2[:, :n], AF.Sqrt, [m2k], [m2k])
                s.op("dve", (lambda e, m2=m2, t0=t0, n=n: e.reciprocal(out=RS[:, t0:t0 + n], in_=m2[:, :n])), [m2k], [("RS", bi)])
                stt(NMR[:, t0:t0 + n], mu[:, :n], -1.0, RS[:, t0:t0 + n], ALU.mult, ALU.mult, [muk, ("RS", bi)], [("NMR", bi)])
            WT = [(WB[0][:, 0:2048], ("wB", 0)), (WB[0][:, 2048:4096], ("wB", 0)),
                  (WB[1][:, 0:2048], ("wB", 1)), (WB[1][:, 2048:4096], ("wB", 1)),
                  (AR[:, 31276:31276 + 2048], ("ARs", 0)), (AR[:, 31276 + 2048:31276 + 4096], ("ARs", 1)),
                  (AR[:, 20480:20480 + 2048], ("dg", 0)), (AR[:, 20480 + 3968:20480 + 3968 + 2048], ("dg", 1))]
            for ft in range(8):
                dma("pool", WT[ft][0], cwout_d[j, ft], f"wt{ft}", writes=[WT[ft][1]], mdl=4096)
            olng, _w = POFF["lng"]
            olnb, _w = POFF["lnb"]
            dma("pool", WA[0][:, 0].rearrange("p k m -> p (k m)"), cwin_d[j, 0, :, 2048:3072], "wA0",
                writes=[("wA", 0)], mdl=4096)
            pend_mult = []
            for ct in range(16):
                sl = ct % 2
                if ct + 1 < 16:
                    dma("pool", WA[1 - sl][:, 0].rearrange("p k m -> p (k m)"), cwin_d[j, ct + 1, :, 2048:3072], f"wA{1 - sl}",
                        writes=[("wA", 1 - sl)], mdl=4096)
                for bi, (t0, n) in enumerate(BLKS):
                    pg, pgk = psn()
                    hk = [("HT", kt, bi) for kt in range(8)]
                    mmg(pg[:, :n], [(WA[sl][:, 0, kt, :], HT[:, kt, t0:t0 + n]) for kt in range(8)], [("wA", sl)] + hk, [pgk])
                    sgb, sgbk = tb()
                    act(sgb[:, :n], pg[:, :n], AF.Silu, [pgk], [sgbk])
                    t1, t1k = tf()
                    tt("dve", t1[:, :n], C[:, ct, t0:t0 + n], RS[:, t0:t0 + n], ALU.mult, [("C", ct, bi), ("RS", bi)], [t1k])
                    tt("dve", t1[:, :n], t1[:, :n], NMR[:, t0:t0 + n], ALU.add, [t1k, ("NMR", bi)], [t1k])
                    yb, ybk = tb()
                    act(yb[:, :n], t1[:, :n], AF.Silu, [t1k, "PRM"], [ybk],
                        scale=PRM[:, olng + j * 16 + ct:olng + j * 16 + ct + 1],
                        bias=PRM[:, olnb + j * 16 + ct:olnb + j * 16 + ct + 1])
                    pend_mult.append((C[:, ct, t0:t0 + n], yb[:, :n], sgb[:, :n], [ybk, sgbk], [("C", ct, bi)]))
                    if len(pend_mult) > 1:
                        o_, a_, b_, r_, w_ = pend_mult.pop(0)
                        tt("dve", o_, a_, b_, ALU.mult, r_, w_)
            o_, a_, b_, r_, w_ = pend_mult.pop(0)
            tt("dve", o_, a_, b_, ALU.mult, r_, w_)
            if l + 1 < n_layers and (l + 1) // 2 not in prep_done:
                ret_prep((l + 1) // 2)
            mod_drain(99)
            snap["s"] = s.snapshot()
            for bi, (t0, n) in enumerate(BLKS):
                cj = 0 if bi < 2 else 1
                for ft in range(8):
                    wt, wtk = WT[ft]
                    wv = wt.rearrange("p (c m) -> p c m", c=16)
                    po, pok = psn()
                    mmg(po[:, :n], [(wv[:, c, :], C[:, c, t0:t0 + n]) for c in range(16)],
                        [wtk] + [("C", c, bi) for c in range(16)], [pok])
                    stt(XT[:, ft, t0:t0 + n], po[:, :n], GTc(ft, cj), XT[:, ft, t0:t0 + n], ALU.mult, ALU.add,
                        [pok, ("MOD", cur["l"]), ("XT", ft, bi)], [("XT", ft, bi)])
                next_rms(bi)

        def ret_layer(l, j):
            R1 = AR[:, 0:4 * T].rearrange("p (r t) -> p r t", r=4)
            V = AR[:, 5120:10240].rearrange("p (t v) -> p t v", t=10)
            SG = AR[:, 10240:15360].rearrange("p (t v) -> p t v", t=10)
            KS = AR[:, 15360:23040].rearrange("p (t w k) -> p t w k", t=10, w=3)
            BST = AR[:, 23040:32256].rearrange("p (r d v) -> p r d v", r=9, d=2)
            FBF = AR[:, 32256:36352].rearrange("p (r d v) -> p r d v", r=4, d=2)
            QFS = AR[:, 36352:36864].rearrange("p (d t) -> p d t", d=2)
            QBS = AR[:, 36864:37376].rearrange("p (d t) -> p d t", d=2)
            COS = ARF[:, 0:1024]
            SIN = ARF[:, 1024:2048]
            DT = ARF[:, 2048:2560].rearrange("p (h i) -> p h i", h=4)
            ARF16 = ARF[:].bitcast(BF16)
            WQF = ARF16[:, 5120:5632].rearrange("p (h i) -> p h i", h=4)
            WQB = ARF16[:, 5632:6144].rearrange("p (h i) -> p h i", h=4)
            isS = P_("isS")
            if j not in prep_done:
                ret_prep(j)
            stage(1)
            if l == 0:
                rms_to_HT()
            stage(2)
            ogn, _w = POFF["gng"]

            def a1_dma(h):
                qk_dma(j, h)

            def a1(h):
                dma("sp", FST[1][:], ext_d[j, 1, h].rearrange("(t p) v -> p t v", p=128), "extb",
                    writes=[("FST", 1, 0), ("FST", 1, 1)])
                dma("sp", FST[0][:], ext_d[j, 0, h].rearrange("(t p) v -> p t v", p=128), "extf",
                    writes=[("FST", 0, 0), ("FST", 0, 1)])
                for which in range(2):
                    kscale = 1.0 if which == 0 else 0.0625
                    for bi, (t0, n) in enumerate(BLKS):
                        hk = [("HT", kt, bi) for kt in range(8)]
                        p1, p1k = psn()
                        p2, p2k = psn()
                        if which == 0:
                            w1, w2, wk1, wk2 = WA[0][:, 0], WA[1][:, 0], ("wA", 0), ("wA", 1)
                        else:
                            w1, w2, wk1, wk2 = ADA[0], ADA[1], ("ADA", 0), ("ADA", 1)
                        mmg(p1[:, :n], [(w1[:, kt, :], HT[:, kt, t0:t0 + n]) for kt in range(8)], [wk1] + hk, [p1k])
                        mmg(p2[:, :n], [(w2[:, kt, :], HT[:, kt, t0:t0 + n]) for kt in range(8)], [wk2] + hk, [p2k])
                        d1 = R1[:, which * 2 + 0, t0:t0 + n]
                        d2 = R1[:, which * 2 + 1, t0:t0 + n]
                        segs = [2 * bi, 2 * bi + 1] if bi < 2 else [4]
                        k1s = [("R1", which * 2 + 0, sg_) for sg_ in segs]
                        k2s = [("R1", which * 2 + 1, sg_) for sg_ in segs]
                        if bi < 2:
                            a_, ak = tf()
                            b_, bk = tf()
                            stt(a_[:, :n], p1[:, :n], kscale, COS[:, t0:t0 + n], ALU.mult, ALU.mult, [p1k, "rope"], [ak])
                            stt(b_[:, :n], p2[:, :n], kscale, SIN[:, t0:t0 + n], ALU.mult, ALU.mult, [p2k, "rope"], [bk])
                            tt("dve", d1, a_[:, :n], b_[:, :n], ALU.subtract, [ak, bk], k1s)
                            c_, ck = tf()
                            d_, dk_ = tf()
                            stt(c_[:, :n], p2[:, :n], kscale, COS[:, t0:t0 + n], ALU.mult, ALU.mult, [p2k, "rope"], [ck])
                            stt(d_[:, :n], p1[:, :n], kscale, SIN[:, t0:t0 + n], ALU.mult, ALU.mult, [p1k, "rope"], [dk_])
                            tt("dve", d2, c_[:, :n], d_[:, :n], ALU.add, [ck, dk_], k2s)
                        else:
                            act(d1, p1[:, :n], AF.Copy, [p1k], k1s, scale=kscale)
                            act(d2, p2[:, :n], AF.Copy, [p2k], k2s, scale=kscale)

            def wb_dma(h):
                for which in range(2):
                    dma("pool", WB[which][:], rwvg_d[j, which * 4 + h], f"wB{which}", writes=[("wB", which)], mdl=4096)

            a1(0)
            for h in range(H):
                stage(3)
                if h == 0:
                    wb_dma(0)

                def proj(which, tti):
                    sl = which
                    wv = WB[sl][:].rearrange("p (k n) -> p k n", k=8)
                    bi = tti // 4
                    hk = [("HT", kt, bi) for kt in range(8)]
                    pv, pvk = psn()
                    mmg(pv[:], [(HT[:, kt, tti * 128:(tti + 1) * 128], wv[:, kt, :]) for kt in range(8)], [("wB", sl)] + hk, [pvk])
                    if which == 0:
                        cpy("dve", V[:, tti, :], pv[:], [pvk], [("V", tti)])
                    else:
                        act(SG[:, tti, :], pv[:], AF.Silu, [pvk], [("SG", tti)])

                PTS = []
                for tti in range(10):
                    proj(0, tti)
                    rg = tti % 2
                    psb = psbs[rg]
                    for dkt in range(2):
                        tr(psb[:, dkt * 128:(dkt + 1) * 128], R1[:, 2 + dkt, tti * 128:(tti + 1) * 128],
                           [("R1", 2 + dkt, tti // 2), "IDB"], [("ps", 6 + rg)], signal=(dkt == 1))
                    src_ = psb[:, 0:256]
                    cols = [0, 1, 2] if tti % 2 == 0 else [0, 2, 3]
                    for w_, cidx in enumerate(cols):
                        col = COLS[:, h, cidx:cidx + 1]
                        act(KS[:, tti, w_, :], src_, AF.Identity, [("ps", 6 + rg), "COLS"], [("KS", tti)], scale=col)
                    cs = slice(tti * 128, (tti + 1) * 128)
                    psS, psSk = psn()
                    mmg(psS[:, 0:128], [(R1[:, 2 + dkt, cs], R1[:, dkt, cs]) for dkt in range(2)],
                        [("R1", i_, tti // 2) for i_ in range(4)], [psSk])
                    pt = TB[tti // 4][:, (tti % 4) * 128:(tti % 4 + 1) * 128]
                    ptk = ("tb", tti // 4)
                    tt("dve", pt, psS[:, 0:128], DT[:, h, :], ALU.mult, [psSk, "DT"], [ptk])
                    PTS.append((pt, ptk))
                    if tti < 6:
                        mod_drain(1)
                stage(5)

                def kv_group(c_first, w_first, c_second, w_second):
                    outp = []
                    for dkt in range(2):
                        pp, ppk = psn()
                        pairs = [(KS[:, c_first, w_first, dkt * 128:(dkt + 1) * 128], V[:, c_first, :])]
                        rd = [("KS", c_first), ("V", c_first)]
                        if c_second is not None:
                            pairs.append((KS[:, c_second, w_second, dkt * 128:(dkt + 1) * 128], V[:, c_second, :]))
                            rd += [("KS", c_second), ("V", c_second)]
                        mmg(pp[:], pairs, rd, [ppk])
                        outp.append((pp, ppk))
                    return outp

                ring["wide"] = True

                def st_out(r, d, FS):
                    dst = st_d[r, j, d, h].rearrange("(t p) v -> p t v", p=128)
                    for dkt in range(2):
                        dma("sp", dst[:, dkt, :], FS[:, dkt, :], f"st{d}{dkt}", reads=[("FST", d, dkt)], is_output=True)

                FF = FST[0]
                QF2 = TF[3][:].bitcast(BF16)
                QSETS = [(QFS, QBS, "Q0"), (QF2[:, 0:512].rearrange("p (d t) -> p d t", d=2),
                                            QF2[:, 512:1024].rearrange("p (d t) -> p d t", d=2), ("tf", 3))]

                def fwd_chain(r):
                    c0, c1 = 2 * r, 2 * r + 1
                    par = r % 2
                    pa = kv_group(c0, 0, None, None)
                    pbf = kv_group(c0, 1, c1, 0)
                    fin_k = ("FBF", par * 2)
                    fc1_k = ("FBF", par * 2 + 1)
                    for dkt in range(2):
                        fk = ("FST", 0, dkt)
                        if r < 4:
                            stt(FBF[:, par * 2 + 1, dkt, :], FF[:, dkt, :], COLS[:, h, 4:5], pa[dkt][0][:], ALU.mult, ALU.add,
                                [fk, "COLS", pa[dkt][1]], [fc1_k])
                        else:
                            cpy("act", FBF[:, par * 2 + 1, dkt, :], pa[dkt][0][:], [pa[dkt][1]], [fc1_k])
                    for dkt in range(2):
                        fk = ("FST", 0, dkt)
                        if r < 4:
                            stt(FF[:, dkt, :], FF[:, dkt, :], COLS[:, h, 5:6], pbf[dkt][0][:], ALU.mult, ALU.add,
                                [fk, "COLS", pbf[dkt][1]], [fk])
                        else:
                            cpy("dve", FF[:, dkt, :], pbf[dkt][0][:], [pbf[dkt][1]], [fk])
                    st_out(r, 0, FF)
                    qf, qb, qk = QSETS[par]
                    s0 = r * SEG
                    for dkt in range(2):
                        qv = R1[:, dkt, s0:s0 + SEG].rearrange("p (c i) -> p c i", c=2)
                        tt("dve", qf[:, dkt, :].rearrange("p (c i) -> p c i", c=2), qv,
                           WQF[:, h, :].unsqueeze(1).broadcast_to([128, 2, 128]), ALU.mult, [("R1", dkt, r), "WQ"], [qk])
                        tt("dve", qb[:, dkt, :].rearrange("p (c i) -> p c i", c=2), qv,
                           WQB[:, h, :].unsqueeze(1).broadcast_to([128, 2, 128]), ALU.mult, [("R1", dkt, r), "WQ"], [qk])

                def o_seg(r):
                    c0, c1 = 2 * r, 2 * r + 1
                    par = r % 2
                    fin_k = ("FBF", par * 2)
                    fc1_k = ("FBF", par * 2 + 1)
                    qf, qb, qk = QSETS[par]
                    for ci, tti in enumerate((c0, c1)):
                        pt, ptk = PTS[tti]
                        pairs = [(pt, V[:, tti, :])]
                        rd = [ptk, ("V", tti), qk]
                        if ci == 0:
                            sf = (FBF[:, par * 2], fin_k) if r < 4 else None
                            sbk = (BST[:, 2 * r + 1], ("BST", 2 * r + 1)) if r < 4 else (BST[:, 8], ("BST", 8))
                        else:
                            sf = (FBF[:, par * 2 + 1], fc1_k)
                            sbk = (BST[:, 2 * r], ("BST", 2 * r)) if r < 4 else None
                        if sf is not None:
                            pairs += [(qf[:, dkt, ci * 128:(ci + 1) * 128], sf[0][:, dkt, :]) for dkt in range(2)]
                            rd.append(sf[1])
                        if sbk is not None:
                            pairs += [(qb[:, dkt, ci * 128:(ci + 1) * 128], sbk[0][:, dkt, :]) for dkt in range(2)]
                            rd.append(sbk[1])
                        po, pok = psn()
                        mmg(po[:], pairs, rd, [pok])
                        b0 = (tti % 4) * 16
                        st6 = SMALL[:, b0:b0 + 6]
                        mv = SMALL[:, b0 + 6:b0 + 8]
                        rsd = SMALL[:, b0 + 8:b0 + 9]
                        smv = ("SMv", tti % 4)
                        smr = ("SMr", tti % 4)
                        s.op("dve", (lambda e, st6=st6, po=po: e.bn_stats(out=st6, in_=po[:])), [pok], [smv])
                        s.op("dve", (lambda e, st6=st6, mv=mv: e.bn_aggr(out=mv, in_=st6)), [smv], [smv])
                        act(rsd, mv[:, 1:2], AF.Sqrt, [smv, "EPSC"], [smr], bias=EPSC, scale=1.0)
                        on, onk = tf()
                        stt(on[:], po[:], mv[:, 0:1], SG[:, tti, :], ALU.subtract, ALU.mult, [pok, smv, ("SG", tti)], [onk])
                        s.op("dve", (lambda e, rsd=rsd: e.reciprocal(out=rsd, in_=rsd)), [smr], [smr])
                        act(SG[:, tti, :], on[:], AF.Identity, [onk, smr], [("SG", tti)], scale=rsd)

                def zt(tti):
                    rg = tti % 2
                    psb = psbs[rg]
                    for c in range(4):
                        tr(psb[:, c * 128:(c + 1) * 128], SG[:, tti, c * 128:(c + 1) * 128],
                           [("SG", tti), "IDB"], [("ps", 6 + rg)], signal=(c == 3))
                    for c in range(4):
                        gc = PRM[:, ogn + j * 16 + h * 4 + c:ogn + j * 16 + h * 4 + c + 1]
                        act(V[:, tti, c * 128:(c + 1) * 128], psb[:, c * 128:(c + 1) * 128], AF.Identity,
                            [("ps", 6 + rg), "PRM"], [("V", tti)], scale=gc)

                def fin_copy(r):
                    par = r % 2
                    for dkt in range(2):
                        act(FBF[:, par * 2, dkt, :], FF[:, dkt, :], AF.Identity, [("FST", 0, dkt), "PRM"], [("FBF", par * 2)], scale=isS)

                FB = FST[1]
                gt = 0
                for dkt in range(2):
                    act(BST[:, 6, dkt, :], FB[:, dkt, :], AF.Identity, [("FST", 1, dkt), "PRM"], [("BST", 6)], scale=isS)
                for r in (3, 2, 1, 0):
                    proj(1, gt)
                    proj(1, gt + 1)
                    gt += 2
                    c0, c1 = 2 * r, 2 * r + 1
                    pa = kv_group(c1, 1, None, None)
                    pbb = kv_group(c1, 2, c0, 2)
                    for dkt in range(2):
                        fk = ("FST", 1, dkt)
                        stt(BST[:, 2 * r + 1, dkt, :], FB[:, dkt, :], COLS[:, h, 6:7], pa[dkt][0][:], ALU.mult, ALU.add,
                            [fk, "COLS", pa[dkt][1]], [("BST", 2 * r + 1)])
                    for dkt in range(2):
                        fk = ("FST", 1, dkt)
                        stt(FB[:, dkt, :], FB[:, dkt, :], COLS[:, h, 7:8], pbb[dkt][0][:], ALU.mult, ALU.add,
                            [fk, "COLS", pbb[dkt][1]], [fk])
                    st_out(r, 1, FB)
                    if r == 2:
                        fin_copy(0)
                        fwd_chain(0)
                        fin_copy(1)
                    if r > 0:
                        for dkt in range(2):
                            act(BST[:, 2 * (r - 1), dkt, :], FB[:, dkt, :], AF.Identity, [("FST", 1, dkt), "PRM"],
                                [("BST", 2 * (r - 1))], scale=isS)
                proj(1, 8)
                proj(1, 9)
                pa = kv_group(9, 1, None, None)
                pbb = kv_group(9, 2, 8, 2)
                for dkt in range(2):
                    cpy("act", BST[:, 8, dkt, :], pa[dkt][0][:], [pa[dkt][1]], [("BST", 8)])
                    cpy("dve", FB[:, dkt, :], pbb[dkt][0][:], [pbb[dkt][1]], [("FST", 1, dkt)])
                st_out(4, 1, FB)

                ring["tfn"] = 3
                ring["wn"] = 6
                if h + 1 < H:
                    a1_dma(h + 1)
                WOUT = AR[:, 15360:15360 + 4096]
                for r in range(NSEG):
                    if r + 1 < NSEG:
                        fwd_chain(r + 1)
                    if r + 1 == NSEG - 1:
                        dma("pool", WOUT, rwout_d[j, h], "wout", writes=[("KS", t_) for t_ in range(10)], mdl=4096)
                        if h + 1 < H:
                            wb_dma(h + 1)
                    o_seg(r)
                    if r + 2 <= 3:
                        fin_copy(r + 2)
                    if r >= 1:
                        zt(2 * r - 2)
                        zt(2 * r - 1)
                ring["wn"] = 8
                ring["tfn"] = 4
                ring["wide"] = False
                if h + 1 < H:
                    a1(h + 1)
                zt(8)
                zt(9)
                stage(7)
                stage(8)
                wv = WOUT.rearrange("p (c n) -> p c n", c=4)
                if h == H - 1:
                    mod_drain(99)
                    snap["s"] = s.snapshot()
                order = [(ft, bi) for ft in range(8) for bi in range(3)] if h < H - 1 else [(ft, bi) for bi in range(3) for ft in range(8)]
                for (ft, bi) in order:
                    if True:
                        t0, n = BLKS[bi]
                        cj = 0 if bi < 2 else 1
                        po, pok = psn()
                        nt = n // 128
                        tt0 = t0 // 128
                        mmg(po[:, :n].rearrange("p (t i) -> p t i", t=nt),
                            [(wv[:, c, ft * 128:(ft + 1) * 128], V[:, tt0:tt0 + nt, c * 128:(c + 1) * 128]) for c in range(4)],
                            [("KS", t_) for t_ in range(10)] + [("V", t_) for t_ in range(tt0, tt0 + nt)], [pok])
                        stt(XT[:, ft, t0:t0 + n], po[:, :n], GTc(ft, cj), XT[:, ft, t0:t0 + n], ALU.mult, ALU.add,
                            [pok, ("MOD", cur["l"]), ("XT", ft, bi)], [("XT", ft, bi)])
                        if h == H - 1 and ft == 7:
                            next_rms(bi)

        ADA0 = AR[:, 0:8 * 3072].rearrange("p (k n) -> p k n", k=8)
        av0 = adaw_d[0].rearrange("(kt p) n -> p kt n", p=128)
        for q in range(6):
            dma("pool", ADA0[:, :, q * 512:(q + 1) * 512], av0[:, :, q * 512:(q + 1) * 512], f"ada0_{q}", writes=[("ADA0", q)])
        for ot in range(24):
            pm, pmk = psn()
            mmg(pm[:, 0:2], [(ADA0[:, kt, ot * 128:(ot + 1) * 128], SCT[:, kt, :]) for kt in range(8)],
                reads=[("ADA0", ot // 4), "SCT"], writes=[pmk])
            o_, _w = POFF["adab"]
            adab = PRM[:, o_ + ot:o_ + ot + 1].broadcast_to([128, 2])
            tt("dve", MODS[0][:, ot, :], pm[:, 0:2], adab, ALU.add, [pmk, "PRM"], [("MOD", 0)])
        mod_finish(0)
        s.barrier()
        for l in range(n_layers):
            cur["l"] = l
            mod_start(l + 1)
            try:
                ring["nar"] = [0, 1, 2, 3, 4, 6, 7] if l % 2 == 0 else [0, 1, 2, 3, 4]
                if l % 2 == 0:
                    conv_layer(l, l // 2)
                else:
                    ret_layer(l, l // 2)
            except _Stop:
                pass
            mod_drain(99)
            if snap["s"] is not None:
                s.barrier_from(snap["s"])
                snap["s"] = None
            else:
                s.barrier()

        with nc.Block() as block:
            s.emit(block)
    return nc


def _core_slots(c):
    if c < 2:
        return [("s", c, r) for r in range(4)] + [("p", c)]
    return [("p", 2 + (c - 2) * 5 + r) for r in range(5)]


_NC_CACHE = {}


def _run(inputs, n_layers=4):
    inp = {k: np.asarray(v) for k, v in inputs.items()}
    f32 = np.float32
    xp, xs = inp["x_prompt"].astype(f32), inp["x_sample"].astype(f32)
    cwin = np.ascontiguousarray(inp["conv_w_in"].reshape(2, 8, 128, 3, 16, 128).transpose(0, 4, 2, 3, 1, 5)).reshape(2, 16, 128, 3072)
    cwout = np.ascontiguousarray(inp["conv_w_out"].reshape(2, 16, 128, 8, 128).transpose(0, 3, 2, 1, 4)).reshape(2, 8, 128, 2048)
    rw = inp["ret_w_in"]
    rwqk = np.ascontiguousarray(rw[:, :, 0:2048].reshape(2, 8, 128, 16, 128).transpose(0, 3, 2, 1, 4)).reshape(2, 16, 128, 1024)
    rwvg = np.ascontiguousarray(rw[:, :, 2048:].reshape(2, 8, 128, 8, 512).transpose(0, 3, 2, 1, 4)).reshape(2, 8, 128, 4096)
    rwout = np.ascontiguousarray(inp["ret_w_out"].reshape(2, 4, 4, 128, 1024).transpose(0, 1, 3, 2, 4)).reshape(2, 4, 128, 4096)
    adaw = np.ascontiguousarray(inp["ada_w"].astype(f32))
    in_maps = []
    for c in range(8):
        slots = _core_slots(c)
        segs = []
        for sl in slots:
            if sl[0] == "s":
                segs.append(xs[sl[1], sl[2] * SEG:(sl[2] + 1) * SEG])
            else:
                segs.append(xp[sl[1]])
        x = np.concatenate(segs, axis=0)
        is_s = c < 2
        condA = inp["c"][c] if is_s else inp["c_ctx"]
        condB = inp["c_ctx"]
        cond = np.stack([condA, condB], axis=-1).reshape(8, 128, 2).transpose(1, 0, 2).reshape(128, 16)
        ext = inp["state_ret"][c if is_s else 0]
        in_maps.append({
            "xT": np.ascontiguousarray(x.T.astype(f32)),
            "cond": np.ascontiguousarray(cond.astype(f32)),
            "prm": _pack_params(inp, 1.0 if is_s else 0.0),
            "rope": _rope_tables(is_s),
            "ext": np.ascontiguousarray(ext.astype(f32)),
            "adaw": adaw, "cwin": cwin, "cwout": cwout, "rwqk": rwqk, "rwvg": rwvg, "rwout": rwout,
        })
    ck = (n_layers, _STOP)
    if ck not in _NC_CACHE:
        _NC_CACHE[ck] = build_program(n_layers)
    nc = _NC_CACHE[ck]
    res = run_bass_kernel_spmd(nc, in_maps, core_ids=list(range(8)))
    y_prompt = np.zeros((32, 256, D), f32)
    y_sample = np.zeros((2, 1024, D), f32)
    new_state = np.zeros((32, 2, 2, 4, 256, 512), f32)
    for c in range(8):
        y = res.results[c]["yT"].T
        stc = res.results[c]["st"]
        for si, sl in enumerate(_core_slots(c)):
            blk = y[si * SEG:(si + 1) * SEG]
            if sl[0] == "s":
                y_sample[sl[1], sl[2] * SEG:(sl[2] + 1) * SEG] = blk
            else:
                y_prompt[sl[1]] = blk
                new_state[sl[1]] = stc[si]
    return y_prompt, y_sample, new_state


def kernel(**inputs):
    return _run(inputs, 4)
```
